# Optimizing a Trainium2 kernel written in Bass

```python
import jax, jax.numpy as jnp
from jax import lax
import numpy as np

D_MODEL = 1024
BATCH = 2
SEQ = 8192
DEPTH = 2

GRID_W = 64
CTX_LEN = 256
N_MIXERS = 2
N_A_LAYERS = (DEPTH + N_MIXERS - 1) // N_MIXERS
N_B_LAYERS = DEPTH // N_MIXERS
EPS = 1e-6
HG_EXPAND = 128
HG_HEADS = D_MODEL // HG_EXPAND
HG_DK = HG_EXPAND
HG_DV = D_MODEL // HG_HEADS
HG_F = HG_HEADS * HG_DK
CHUNK = 64
D_RNN = D_MODEL
LRU_BLOCKS = 4
LRU_BW = D_RNN // LRU_BLOCKS
LRU_CONV_W = 4
LRU_C = 8.0
D_FF = 2816
FFN_CONV_W = 3

kernel_name = "hybrid_hgrn2_rglru_convffn_prefix_dit"


def rmsnorm(x, g):
    xf = x.astype(jnp.float32)
    y = xf * lax.rsqrt(jnp.mean(xf * xf, axis=-1, keepdims=True) + EPS)
    return (y * g).astype(x.dtype)


def modulate(h, shift, scale):
    return h * (1.0 + scale) + shift


def dwconv(x, w, b):
    K = w.shape[0]
    right = (K - 1) // 2
    left = K - 1 - right
    T = x.shape[1]
    xp = jnp.pad(x, ((0, 0), (left, right), (0, 0)))
    return sum(w[k] * xp[:, k:k + T] for k in range(K)) + b


def _flip(t, d, axis):
    return t if d == 0 else jnp.flip(t, axis=axis)


def hgrn2_chunk_scan(q, k, v, logf, s0):
    Bn, H, T, _ = q.shape
    n = T // CHUNK
    rs = lambda t: t.reshape(Bn, H, n, CHUNK, t.shape[-1])
    q, k, v, logf = rs(q), rs(k), rs(v), rs(logf)
    b = jnp.cumsum(logf, axis=3)
    b_ref = b[:, :, :, CHUNK // 2:CHUNK // 2 + 1]
    b_last = b[:, :, :, -1:]
    qe = q * jnp.exp(b - b_ref)
    ke = k * jnp.exp(b_ref - b)
    scores = jnp.einsum('bhnik,bhnjk->bhnij', qe, ke)
    mask = jnp.tril(jnp.ones((CHUNK, CHUNK), dtype=bool))
    scores = jnp.where(mask, scores, 0.0)
    o_intra = jnp.einsum('bhnij,bhnjv->bhniv', scores, v)
    kv = jnp.einsum('bhnjk,bhnjv->bhnkv', k * jnp.exp(b_last - b), v)
    decay = jnp.exp(b_last[:, :, :, 0])

    def step(s, inp):
        dcy, u = inp
        return dcy[..., None] * s + u, s

    s_final, s_start = lax.scan(step, s0, (jnp.moveaxis(decay, 2, 0), jnp.moveaxis(kv, 2, 0)))
    s_start = jnp.moveaxis(s_start, 0, 2)
    o_inter = jnp.einsum('bhnik,bhnkv->bhniv', q * jnp.exp(b), s_start)
    o = (o_intra + o_inter).reshape(Bn, H, T, v.shape[-1])
    return o, s_final


def hgrn2_mixer(h_ctx, h_lat, w_in, lb, g_gain, w_out, need_ctx):
    def prep(h):
        Bn, T, _ = h.shape
        heads = lambda t: t.reshape(Bn, T, HG_HEADS, -1).transpose(0, 2, 1, 3)
        q, zf_fwd, zf_bwd, v, g = jnp.split(h @ w_in, 5, axis=-1)
        q = heads(jax.nn.silu(q))
        v = heads(v)
        logf, k = [], []
        for d, zf in enumerate((zf_fwd, zf_bwd)):
            f = lb[d] + (1.0 - lb[d]) * jax.nn.sigmoid(zf.astype(jnp.float32))
            logf.append(heads(jnp.log(f)))
            k.append(heads(1.0 - f))
        return q, k, v, logf, g

    qc, kc, vc, lfc, gc = prep(h_ctx)
    ql, kl, vl, lfl, gl = prep(h_lat)
    s0 = jnp.zeros((h_lat.shape[0], HG_HEADS, HG_DK, HG_DV), jnp.float32)
    o_c, o_l = 0.0, 0.0
    for d in range(2):
        oc_d, s_ctx = hgrn2_chunk_scan(_flip(qc, d, 2), _flip(kc[d], d, 2), _flip(vc, d, 2),
                                       _flip(lfc[d], d, 2), s0)
        ol_d, _ = hgrn2_chunk_scan(_flip(ql, d, 2), _flip(kl[d], d, 2), _flip(vl, d, 2),
                                   _flip(lfl[d], d, 2), s_ctx)
        o_c = o_c + _flip(oc_d, d, 2)
        o_l = o_l + _flip(ol_d, d, 2)

    def readout(o, g):
        Bn, H, T, DV = o.shape
        o = rmsnorm(o, g_gain).transpose(0, 2, 1, 3).reshape(Bn, T, H * DV)
        return (o * jax.nn.silu(g)) @ w_out

    y_c = readout(o_c, gc) if need_ctx else None
    return y_c, readout(o_l, gl)


def lru_scan(log_a, u, h0):
    a = jnp.exp(log_a)
    u = u.at[:, 0].add(a[:, 0] * h0)

    def combine(l, r):
        al, ul = l
        ar, ur = r
        return al * ar, ar * ul + ur

    _, h = lax.associative_scan(combine, (a, u), axis=1)
    return h


def rglru_mixer(h_ctx, h_lat, w_in, conv_w, conv_b, wa, ba, wx, bx, lam, w_out, need_ctx):
    Bn, T, _ = h_lat.shape
    rows = T // GRID_W
    to_cm = lambda t: t.reshape(Bn, rows, GRID_W, -1).transpose(0, 2, 1, 3).reshape(Bn, T, -1)
    from_cm = lambda t: t.reshape(Bn, GRID_W, rows, -1).transpose(0, 2, 1, 3).reshape(Bn, T, -1)

    def branch(h):
        gate, xr = jnp.split(h @ w_in, 2, axis=-1)
        return gate, dwconv(xr, conv_w, conv_b)

    def gates(xr, d):
        xb = xr.reshape(*xr.shape[:-1], LRU_BLOCKS, LRU_BW)
        r = jax.nn.sigmoid(jnp.einsum('btnk,nkj->btnj', xb, wa[d]).reshape(xr.shape) + ba[d])
        i = jax.nn.sigmoid(jnp.einsum('btnk,nkj->btnj', xb, wx[d]).reshape(xr.shape) + bx[d])
        log_a = -LRU_C * r.astype(jnp.float32) * jax.nn.softplus(-lam[d].astype(jnp.float32))
        mult = jnp.sqrt(-jnp.expm1(2.0 * log_a))
        return log_a, mult * (i * xr)

    gate_c, xc = branch(h_ctx)
    gate_l, xl = branch(to_cm(h_lat))
    h0 = jnp.zeros((Bn, D_RNN), jnp.float32)
    y_c, y_l = 0.0, 0.0
    for d in range(2):
        la_c, u_c = gates(_flip(xc, d, 1), d)
        hc = lru_scan(la_c, u_c, h0)
        la_l, u_l = gates(_flip(xl, d, 1), d)
        hl = lru_scan(la_l, u_l, hc[:, -1])
        y_c = y_c + _flip(hc, d, 1)
        y_l = y_l + _flip(hl, d, 1)
    out_l = from_cm(y_l * jax.nn.gelu(gate_l)) @ w_out
    out_c = (y_c * jax.nn.gelu(gate_c)) @ w_out if need_ctx else None
    return out_c, out_l


def conv_ffn(h, w_up, conv_w, conv_b, w_down):
    u = dwconv(h @ w_up, conv_w, conv_b)
    gate, val = jnp.split(u, 2, axis=-1)
    return (jax.nn.silu(gate) * val) @ w_down


def setup_inputs(seed: int = 0) -> dict:
    key = jax.random.key(seed)
    ks = iter(jax.random.split(key, 32))
    nrm = lambda shape, s: jax.random.normal(next(ks), shape, jnp.float32) * s
    D = D_MODEL
    a0 = jax.random.uniform(next(ks), (N_B_LAYERS, 2, D_RNN), jnp.float32, 0.9, 0.999)
    sig = a0 ** (1.0 / LRU_C)
    lam = jnp.log(sig) - jnp.log1p(-sig)
    return {
        "x": nrm((BATCH, SEQ, D), 1.0),
        "c": nrm((BATCH, D), 1.0),
        "ctx": nrm((BATCH, CTX_LEN, D), 1.0),
        "c_ctx": nrm((D,), 1.0),
        "w_ada": nrm((DEPTH, D, 6 * D), D ** -0.5),
        "b_ada": nrm((DEPTH, 6 * D), 0.02),
        "norm_g": 1.0 + nrm((DEPTH, 2, D), 0.02),
        "hg_w_in": nrm((N_A_LAYERS, D, 5 * D), D ** -0.5),
        "hg_lb_logits": nrm((N_A_LAYERS + 1, 2, HG_F), 0.1),
        "hg_gnorm": 1.0 + nrm((N_A_LAYERS, HG_DV), 0.02),
        "hg_w_out": nrm((N_A_LAYERS, D, D), D ** -0.5),
        "lru_w_in": nrm((N_B_LAYERS, D, 2 * D_RNN), D ** -0.5),
        "lru_conv_w": nrm((N_B_LAYERS, LRU_CONV_W, D_RNN), LRU_CONV_W ** -0.5),
        "lru_conv_b": nrm((N_B_LAYERS, D_RNN), 0.02),
        "lru_wa": nrm((N_B_LAYERS, 2, LRU_BLOCKS, LRU_BW, LRU_BW), LRU_BW ** -0.5),
        "lru_ba": nrm((N_B_LAYERS, 2, D_RNN), 0.02),
        "lru_wx": nrm((N_B_LAYERS, 2, LRU_BLOCKS, LRU_BW, LRU_BW), LRU_BW ** -0.5),
        "lru_bx": nrm((N_B_LAYERS, 2, D_RNN), 0.02),
        "lru_lambda": lam,
        "lru_w_out": nrm((N_B_LAYERS, D_RNN, D), D_RNN ** -0.5),
        "ffn_w_up": nrm((DEPTH, D, 2 * D_FF), D ** -0.5),
        "ffn_conv_w": nrm((DEPTH, FFN_CONV_W, 2 * D_FF), FFN_CONV_W ** -0.5),
        "ffn_conv_b": nrm((DEPTH, 2 * D_FF), 0.02),
        "ffn_w_down": nrm((DEPTH, D_FF, D), D_FF ** -0.5),
        "final_g": 1.0 + nrm((D,), 0.02),
    }


def reference(x, c, ctx, c_ctx, w_ada, b_ada, norm_g, hg_w_in, hg_lb_logits, hg_gnorm, hg_w_out,
              lru_w_in, lru_conv_w, lru_conv_b, lru_wa, lru_ba, lru_wx, lru_bx, lru_lambda, lru_w_out,
              ffn_w_up, ffn_conv_w, ffn_conv_b, ffn_w_down, final_g):
    lower_bounds = jnp.cumsum(jax.nn.softmax(hg_lb_logits.astype(jnp.float32), axis=0), axis=0)
    lat, cx = x, ctx
    sc_lat, sc_ctx = jax.nn.silu(c), jax.nn.silu(c_ctx)
    for l in range(DEPTH):
        last = l == DEPTH - 1
        m_l = (sc_lat @ w_ada[l] + b_ada[l])[:, None, :]
        m_c = sc_ctx @ w_ada[l] + b_ada[l]
        sh1, sc1, g1, sh2, sc2, g2 = jnp.split(m_l, 6, axis=-1)
        csh1, csc1, cg1, csh2, csc2, cg2 = jnp.split(m_c, 6, axis=-1)
        hl = modulate(rmsnorm(lat, norm_g[l, 0]), sh1, sc1)
        hc = modulate(rmsnorm(cx, norm_g[l, 0]), csh1, csc1)
        j = l // N_MIXERS
        if l % N_MIXERS == 0:
            yc, yl = hgrn2_mixer(hc, hl, hg_w_in[j], lower_bounds[j], hg_gnorm[j], hg_w_out[j],
                                 not last)
        else:
            yc, yl = rglru_mixer(hc, hl, lru_w_in[j], lru_conv_w[j], lru_conv_b[j], lru_wa[j],
                                 lru_ba[j], lru_wx[j], lru_bx[j], lru_lambda[j], lru_w_out[j],
                                 not last)
        lat = lat + g1 * yl
        hl = modulate(rmsnorm(lat, norm_g[l, 1]), sh2, sc2)
        lat = lat + g2 * conv_ffn(hl, ffn_w_up[l], ffn_conv_w[l], ffn_conv_b[l], ffn_w_down[l])
        if not last:
            cx = cx + cg1 * yc
            hc = modulate(rmsnorm(cx, norm_g[l, 1]), csh2, csc2)
            cx = cx + cg2 * conv_ffn(hc, ffn_w_up[l], ffn_conv_w[l], ffn_conv_b[l], ffn_w_down[l])
    return rmsnorm(lat, final_g)
```

```python
import numpy as np
from contextlib import ExitStack
import concourse.bass as bass
import concourse.mybir as mybir
from concourse.bass_utils import run_bass_kernel_spmd

F32 = mybir.dt.float32
BF16 = mybir.dt.bfloat16
AF = mybir.ActivationFunctionType
ALU = mybir.AluOpType
AX = mybir.AxisListType


class Tl:
    __slots__ = ("ap", "w", "r", "name")

    def __init__(self, ap, name=""):
        self.ap = ap
        self.w = {}
        self.r = {}
        self.name = name

    def __getitem__(self, idx):
        return Vw(self, self.ap[idx])

    def v(self, ap):
        return Vw(self, ap)


class Vw:
    __slots__ = ("t", "ap")

    def __init__(self, t, ap):
        self.t = t
        self.ap = ap

    def __getitem__(self, idx):
        return Vw(self.t, self.ap[idx])

    def re(self, pat, **kw):
        return Vw(self.t, self.ap.rearrange(pat, **kw))


def _ap(x):
    return x.ap if isinstance(x, Vw) else x


class Prog:
    ENGS = ("pe", "act", "dve", "pool", "sp")

    def __init__(self, nc):
        self.nc = nc
        self.es = ExitStack()
        self.es_cur = self.es
        self.q = {e: [] for e in self.ENGS}
        self.cnt = {}
        self.sems = {}
        self.waited = {}
        self.ring = {"sp": 12, "pool": 6, "act": 6}
        self.ringpos = {k: 0 for k in self.ring}
        self.final = []
        for e in ("pe", "act", "dve", "pool"):
            self.sems[e] = self.es.enter_context(nc.semaphore("s_" + e))
            self.cnt[e] = 0
        for qn, k in self.ring.items():
            for j in range(k):
                key = ("d", qn, j)
                self.sems[key] = self.es.enter_context(nc.semaphore("d_%s%d" % (qn, j)))
                self.cnt[key] = 0
        self.nops = 0
        self.cc_inc = 1

    def barrier(self):
        cur = dict(self.cnt)
        for e in self.ENGS:
            waits = []
            for k, n in cur.items():
                if n and not (k == "pe" and e == "pe") and self.waited.get((e, k), 0) < n:
                    self.waited[(e, k)] = n
                    waits.append((self.sems[k], n))

            def run(eng, waits=waits):
                for s_, n in waits:
                    eng.wait_ge(s_, n)

            self.q[e].append(run)

    def scope(self):
        prog = self

        class _S:
            def __enter__(self_):
                self_.old = prog.es_cur
                prog.es_cur = ExitStack()
                return self_

            def __exit__(self_, *a):
                prog.barrier()
                prog.es_cur.close()
                prog.es_cur = self_.old
                return False

        return _S()

    def sb(self, name, shape, dt=F32):
        self.nalloc = getattr(self, "nalloc", 0) + 1
        t = self.es_cur.enter_context(self.nc.sbuf_tensor("sb%d_%s" % (self.nalloc, name), list(shape), dt))
        return Tl(t[:], name)

    def ps(self, name, shape, dt=F32):
        t = self.es.enter_context(self.nc.psum_tensor("ps_" + name, list(shape), dt))
        return Tl(t[:], name)

    def dram(self, name, shape, dt=F32, kind="Internal"):
        t = self.nc.dram_tensor(name, list(shape), dt, kind=kind)
        return Tl(t.ap(), ("D:" if kind == "ExternalOutput" else "d:") + name)

    def _op(self, eng, fn, reads, writes, dma=False, dinc=16, key=None):
        deps = {}
        for v in reads:
            t = v.t
            for k, n in t.w.items():
                if deps.get(k, 0) < n:
                    deps[k] = n
        for v in writes:
            t = v.t
            for k, n in t.w.items():
                if deps.get(k, 0) < n:
                    deps[k] = n
            for k, n in t.r.items():
                if deps.get(k, 0) < n:
                    deps[k] = n
        if key is not None:
            prev = self.cnt[key]
            self.cnt[key] = prev + dinc
            inc = dinc
        elif dma:
            j = self.ringpos[eng]
            self.ringpos[eng] = (j + 1) % self.ring[eng]
            key = ("d", eng, j)
            prev = self.cnt[key]
            if prev and deps.get(key, 0) < prev:
                deps[key] = prev
            self.cnt[key] = prev + dinc
            inc = dinc
        else:
            key = eng
            self.cnt[key] += 1
            inc = 1
        tok = self.cnt[key]
        waits = []
        for k, n in deps.items():
            if k == "pe" and eng == "pe":
                continue
            if self.waited.get((eng, k), 0) >= n:
                continue
            self.waited[(eng, k)] = n
            waits.append((self.sems[k], n))
        sem = self.sems[key]

        def run(e, waits=waits, fn=fn, sem=sem, inc=inc):
            for s, n in waits:
                e.wait_ge(s, n)
            fn(e).then_inc(sem, inc)

        self.q[eng].append(run)
        self.nops += 1
        for v in reads:
            v.t.r[key] = tok
        for v in writes:
            v.t.w[key] = tok
        return (key, tok)

    def mm(self, out, lhsT, rhs, start=True, stop=True):
        o, l, r = out.ap, lhsT.ap, rhs.ap
        return self._op("pe", lambda e: e.matmul(o, l, r, start=start, stop=stop), [lhsT, rhs], [out])

    def tr(self, out, in_, ident):
        o, i, d = out.ap, in_.ap, ident.ap
        return self._op("pe", lambda e: e.transpose(o, i, d), [in_, ident], [out])

    def act(self, out, in_, func, bias=0.0, scale=1.0, accum=None, eng="act"):
        rd = [in_] + [x for x in (bias, scale) if isinstance(x, Vw)]
        wr = [out] + ([accum] if accum is not None else [])
        o, i, b, s = out.ap, in_.ap, _ap(bias), _ap(scale)
        if accum is not None:
            a = accum.ap
            return self._op(eng, lambda e: e.activation(o, i, func, bias=b, scale=s, accum_out=a), rd, wr)
        return self._op(eng, lambda e: e.activation(o, i, func, bias=b, scale=s), rd, wr)

    def tt(self, eng, out, in0, in1, op):
        o, a, b = out.ap, in0.ap, in1.ap
        return self._op(eng, lambda e: e.tensor_tensor(o, a, b, op), [in0, in1], [out])

    def ts(self, eng, out, in0, s1, s2=None, op0=ALU.mult, op1=None, accum=None):
        rd = [in0] + [x for x in (s1, s2) if isinstance(x, Vw)]
        wr = [out] + ([accum] if accum is not None else [])
        o, a, x1, x2 = out.ap, in0.ap, _ap(s1), _ap(s2)
        kw = {}
        if op1 is not None:
            kw["op1"] = op1
        if accum is not None:
            kw["accum_out"] = accum.ap
        return self._op(eng, lambda e: e.tensor_scalar(o, a, x1, x2, op0, **kw), rd, wr)

    def stt(self, eng, out, in0, scalar, in1, op0, op1):
        rd = [in0, in1] + ([scalar] if isinstance(scalar, Vw) else [])
        o, a, s, b = out.ap, in0.ap, _ap(scalar), in1.ap
        return self._op(eng, lambda e: e.scalar_tensor_tensor(o, a, s, b, op0, op1), rd, [out])

    def scan(self, eng, out, d0, d1, init, op0, op1):
        rd = [d0, d1] + ([init] if isinstance(init, Vw) else [])
        o, a, b, i = out.ap, d0.ap, d1.ap, _ap(init)
        return self._op(eng, lambda e: e.tensor_tensor_scan(o, a, b, i, op0, op1), rd, [out])

    def copy(self, eng, out, in_):
        o, i = out.ap, in_.ap
        if eng == "act":
            return self._op(eng, lambda e: e.copy(o, i), [in_], [out])
        return self._op(eng, lambda e: e.tensor_copy(o, i), [in_], [out])

    def memset(self, eng, out, val):
        o = out.ap
        return self._op(eng, lambda e: e.memset(o, val), [], [out])

    def recip(self, out, in_):
        o, i = out.ap, in_.ap
        return self._op("dve", lambda e: e.reciprocal(o, i), [in_], [out])

    def reduce(self, eng, out, in_, op=ALU.add, axis=AX.X):
        o, i = out.ap, in_.ap
        return self._op(eng, lambda e: e.tensor_reduce(o, i, axis, op), [in_], [out])

    def iota(self, out, pattern, base=0, cm=0):
        o = out.ap
        return self._op("pool", lambda e: e.iota(o, pattern, base=base, channel_multiplier=cm), [], [out])

    def affine_select(self, out, in_, pattern, cmp, fill, base=0, cm=0):
        o, i = out.ap, in_.ap
        return self._op("pool", lambda e: e.affine_select(o, i, pattern, cmp, fill, base=base, channel_multiplier=cm), [in_], [out])

    def dma(self, out, in_, q="sp", final=False, **kw):
        o, i = out.ap, in_.ap
        tok = self._op(q, lambda e: e.dma_start(o, i, **kw), [in_], [out], dma=True)
        if final or getattr(out.t, "name", "").startswith("D:"):
            self.final.append(tok)
        return tok

    def allgather(self, out, in_, ncores=8):
        o, i = out.ap, in_.ap
        groups = [list(range(ncores))]
        self.barrier()
        idx = getattr(self, "ncc", 0)
        self.ncc = idx + 1
        key = ("cc", idx)
        self.sems[key] = self.es.enter_context(self.nc.semaphore("cc_%d" % idx))
        self.cnt[key] = 0
        tok = self._op("pool", lambda e: e.collective_compute("AllGather", ALU.bypass, replica_groups=groups, ins=[i.opt()], outs=[o.opt()]), [in_], [out], dma=True, dinc=1, key=key)
        self.barrier()
        return tok

    def emit(self):
        nc = self.nc
        fin = list(self.final)
        sems = self.sems

        def finrun(e):
            for k, n in fin:
                e.wait_ge(sems[k], n)

        self.q["sp"].append(finrun)
        q = self.q
        with nc.Block() as block:
            @block.tensor
            def _(e):
                for f in q["pe"]:
                    f(e)

            @block.scalar
            def _(e):
                for f in q["act"]:
                    f(e)

            @block.vector
            def _(e):
                for f in q["dve"]:
                    f(e)

            @block.gpsimd
            def _(e):
                for f in q["pool"]:
                    f(e)

            @block.sync
            def _(e):
                for f in q["sp"]:
                    f(e)
        self.es.close()


NT, NCX, NL = 2304, 256, 2048
EPS = 1e-6
BLK5 = [(0, 256), (256, 768), (768, 1280), (1280, 1792), (1792, 2304)]


class PV:
    def __init__(self):
        self.off = {}
        self.cols = []
        self.n = 0

    def add(self, name, arr):
        arr = np.ascontiguousarray(arr, dtype=np.float32).reshape(128, -1)
        self.off[name] = (self.n, arr.shape[1])
        self.cols.append(arr)
        self.n += arr.shape[1]

    def pack(self):
        return np.ascontiguousarray(np.concatenate(self.cols, axis=1))


def fm(v):
    v = np.asarray(v)
    lead = v.shape[:-1]
    C = v.shape[-1] // 128
    v = v.reshape(lead + (C, 128))
    return np.moveaxis(v, -1, 0)


def pv_layout(inputs, core):
    b, k = core // 4, core % 4
    pv = PV()
    I = inputs
    for l in range(2):
        for i in range(2):
            pv.add("ng%d%d" % (l, i), fm(I["norm_g"][l, i]))
    pv.add("fg", fm(I["final_g"]))
    for l in range(2):
        pv.add("bada%d" % l, fm(I["b_ada"][l]))
    pv.add("c", np.stack([fm(I["c"][b]), fm(I["c_ctx"])], axis=-1))
    pv.add("lbl", fm(I["hg_lb_logits"]))
    pv.add("gn", np.asarray(I["hg_gnorm"][0]).reshape(128, 1))
    pv.add("cw", fm(I["lru_conv_w"][0]))
    pv.add("cb", fm(I["lru_conv_b"][0]))
    pv.add("ba", fm(I["lru_ba"][0]))
    pv.add("bx", fm(I["lru_bx"][0]))
    pv.add("lam", fm(I["lru_lambda"][0]))
    for l in range(2):
        pv.add("fcw%d" % l, fm(I["ffn_conv_w"][l]))
        pv.add("fcb%d" % l, fm(I["ffn_conv_b"][l]))
    ranks = np.arange(8)
    same = (ranks // 4) == b
    rk = ranks % 4
    m = np.zeros((10, 8), np.float32)
    m[0] = same & (rk < k)
    m[1] = 1.0 - m[0]
    m[2] = same & (rk > k)
    m[3] = 1.0 - m[2]
    m[4] = same & (rk == k - 1)
    m[5] = same & (rk == k + 1)
    m[6] = same & (rk == (k - 1) % 4)
    m[7] = same & (rk == (k + 1) % 4)
    m[8] = same & (rk == k)
    m[9] = same
    pv.add("msk", np.broadcast_to(m.reshape(1, 80), (128, 80)))
    e = np.array([k > 0, k < 3, k == 0, k == 3, k != 0, k != 3], np.float32)
    pv.add("edge", np.broadcast_to(e.reshape(1, 6), (128, 6)))
    kk = np.zeros(4, np.float32); kk[k] = 1.0
    pv.add("k1h", np.broadcast_to(kk.reshape(1, 4), (128, 4)))
    return pv


class Bld:
    def __init__(self, stages, fused, pvoff, nv):
        self.stages = set(stages)
        self.fused = fused
        self.nc = bass.Bass("TRN2", target_bir_lowering=False)
        self.P = Prog(self.nc)
        self.ins = []
        self.outs = []
        self.pvoff = pvoff
        self.nv = nv
        self.xts = {}

    def xin(self, name, shape, dt=F32):
        if name not in self.xts:
            self.xts[name] = self.P.dram(name, shape, dt, kind="ExternalInput")
            self.ins.append(name)
        return self.xts[name]

    def xt(self, name, shape, dt, prod):
        if name in self.xts:
            return self.xts[name]
        if self.fused or name in getattr(self, "internal", ()):
            kind = "Internal"
        elif prod in self.stages:
            kind = "ExternalOutput"
            self.outs.append(name)
        else:
            kind = "ExternalInput"
            self.ins.append(name)
        self.xts[name] = self.P.dram(name, shape, dt, kind=kind)
        return self.xts[name]

    def pvv(self, name):
        o, n = self.pvoff[name]
        return self.PVt[:, o:o + n]


def setup_common(B):
    P = B.P
    B.PVt = P.sb("pvec", [128, B.nv])
    pvin = B.xin("pvec_in", [128, B.nv])
    P.dma(B.PVt[:], pvin[:])
    B.ones_bf = P.sb("ones_bf", [128, 128], BF16)
    P.memset("pool", B.ones_bf[:], 1.0)
    idf = P.sb("idf", [128, 128], F32)
    P.memset("pool", idf[:], 1.0)
    P.affine_select(idf[:], idf[:], [[-1, 128]], ALU.is_equal, 0.0, base=0, cm=1)
    B.ident_bf = P.sb("ident_bf", [128, 128], BF16)
    P.copy("dve", B.ident_bf[:], idf[:])
    B.ident_f = idf
    B.pb = [P.ps("pb%d" % i, [128, 512], F32) for i in range(7)]
    B.pbb = P.ps("pbb", [128, 1024], BF16)


def mods_layer(B, l, w_ada, mods, A):
    P = B.P
    scT = P.sb("scT%d" % l, [128, 8, 2])
    P.act(scT[:], B.pvv("c").re("p (c t) -> p c t", t=2), AF.Silu)
    pm = B.pb[0]
    wst = [P.sb("wada_st%d_%d" % (l, i), [128, 8, 768]) for i in range(2)]
    for g in range(8):
        st = wst[g % 2]
        P.dma(st[:], w_ada[l, :, g * 768:(g + 1) * 768].re("(kc p) n -> p kc n", p=128), q=("sp" if g % 2 == 0 else "pool"))
        for j in range(6):
            n = g * 6 + j
            for kc in range(8):
                P.mm(pm[:, n * 2:n * 2 + 2], st[:, kc, j * 128:(j + 1) * 128], scT[:, kc, :], start=(kc == 0), stop=(kc == 7))
    bada = B.pvv("bada%d" % l)
    for t in range(2):
        P.tt("dve", mods[:, :, t], pm[:, 0:96].re("p (n t) -> p n t", t=2)[:, :, t], bada, ALU.add)
    for i, base in ((0, 8), (1, 32)):
        for t in range(2):
            P.stt("dve", A[:, i, :, t], mods[:, base:base + 8, t], 1.0, B.pvv("ng%d%d" % (l, i)), ALU.add, ALU.mult)
    return mods, A


def norm_mod(B, Xd, c0, c1, A, Sh, out, o0, xst, tag):
    P = B.P
    w = c1 - c0
    P.dma(xst[:, :, 0:w], Xd[:, c0:c1].re("(kc p) t -> p kc t", p=128))
    sq = B.nm_sq
    for kc in range(8):
        P.act(sq[:, kc, 0:w], xst[:, kc, 0:w], AF.Square)
    ps = B.pb[1]
    for kc in range(8):
        P.mm(ps[:, 0:w], B.ones_bf[:], sq[:, kc, 0:w], start=(kc == 0), stop=(kc == 7))
    rstd = B.nm_rstd
    P.ts("dve", rstd[:, 0:w], ps[:, 0:w], 1.0 / 1024.0, EPS, ALU.mult, ALU.add)
    P.act(rstd[:, 0:w], rstd[:, 0:w], AF.Sqrt)
    P.recip(rstd[:, 0:w], rstd[:, 0:w])
    tmp = B.nm_tmp
    for kc in range(8):
        P.stt("dve", tmp[:, kc % 2, 0:w], xst[:, kc, 0:w], A[:, kc:kc + 1], rstd[:, 0:w], ALU.mult, ALU.mult)
        P.act(out[:, kc, o0:o0 + w], tmp[:, kc % 2, 0:w], AF.Identity, bias=Sh[:, kc:kc + 1])


def norm_bufs(B):
    P = B.P
    B.nm_sq = P.sb("nm_sq", [128, 8, 512], BF16)
    B.nm_rstd = P.sb("nm_rstd", [128, 512])
    B.nm_tmp = P.sb("nm_tmp", [128, 2, 512])
    B.xst = P.sb("xst", [128, 8, 512])

def stage1(B, I):
    P = B.P
    Xd = B.xin("XT0", [1024, NT])
    w_ada = B.xin("w_ada", [2, 1024, 6144])
    hgw = B.xin("hg_w", [8, 128, 8 * 5 * 128])
    mods = P.sb("mods0t", [128, 48, 2])
    A = P.sb("modA0", [128, 2, 8, 2])
    with P.scope():
        mods_layer(B, 0, w_ada, mods, A)
    modsd = B.xt("mods0", [128, 96], F32, 1)
    P.dma(modsd[:], mods[:].re("p n t -> p (n t)"))
    H1 = P.sb("H1", [128, 8, NT], BF16)
    with P.scope():
        norm_bufs(B)
        for (c0, c1) in BLK5:
            t = 1 if c0 < NCX else 0
            norm_mod(B, Xd, c0, c1, A[:, 0, :, t], mods[:, 0:8, t], H1, c0, B.xst, "h1")
    lbl = B.pvv("lbl").re("p (a d h) -> p a d h", a=2, d=2)
    lb = P.sb("lb", [128, 2, 8]); oml = P.sb("oml", [128, 2, 8]); noml = P.sb("noml", [128, 2, 8])
    P.tt("dve", lb[:], lbl[:, 0], lbl[:, 1], ALU.subtract)
    P.act(lb[:], lb[:], AF.Sigmoid)
    P.ts("dve", oml[:], lb[:], -1.0, 1.0, ALU.mult, ALU.add)
    P.ts("dve", noml[:], oml[:], -1.0, None, ALU.mult)
    cm = P.sb("cm", [128, NT], F32)
    P.memset("pool", cm[:], 1.0)
    P.memset("pool", cm[:].re("p (n j) -> p n j", j=64)[:, :, 0:1], 0.0)
    ones32 = P.sb("ones32", [128, 1024], F32)
    P.memset("pool", ones32[:], 1.0)
    Mk = []
    for d in range(2):
        m = P.sb("Mk%d" % d, [128, 128], F32)
        P.memset("pool", m[:], 1.0)
        if d == 0:
            P.affine_select(m[:], m[:], [[1, 128]], ALU.is_ge, 0.0, base=0, cm=-1)
            P.memset("pool", m[0:64, 64:128], 0.0)
        else:
            P.affine_select(m[:], m[:], [[-1, 128]], ALU.is_ge, 0.0, base=0, cm=1)
            P.memset("pool", m[64:128, 0:64], 0.0)
        Mk.append(m)
    wst = [P.sb("hw_st%d" % i, [128, 5 * 128]) for i in range(2)]
    wbf = [P.sb("hw_bf%d" % i, [128, 8, 5, 128], BF16) for i in range(2)]
    qb = P.sb("qb", [128, NT], BF16)
    sg = P.sb("sg", [128, NT], BF16)
    vtok = P.sb("vtok", [128, 18, 128], BF16)
    W = 1024
    T = [P.sb("tmp%d" % i, [128, W]) for i in range(5)]
    qe = [P.sb("qe%d" % d, [128, NT], BF16) for d in range(2)]
    ke = [P.sb("ke%d" % d, [128, NT], BF16) for d in range(2)]
    qd = [P.sb("qd%d" % d, [128, NT], BF16) for d in range(2)]
    kd = [P.sb("kd%d" % d, [128, NT], BF16) for d in range(2)]
    dec = [P.sb("dec%d" % d, [128, 36]) for d in range(2)]
    qt = [P.sb("qt%d" % d, [128, NL], BF16) for d in range(2)]
    bgl = [P.sb("bgl%d" % d, [128, 1]) for d in range(2)]
    oacc = P.sb("oacc", [128, NT])
    S = [P.sb("S%d" % d, [128, 128]) for d in range(2)]
    Sbf = [P.sb("Sbf%d" % d, [128, 128], BF16) for d in range(2)]
    kdT = [P.sb("kdT%d" % d, [128, 128], BF16) for d in range(2)]
    sT = [P.sb("sT%d" % d, [128, 128], BF16) for d in range(2)]
    EPt = P.sb("EPt", [128, 2, 129])
    Sct = P.sb("Sct", [128, 2, 128])
    Yc = P.sb("Yc", [128, 8, NCX], BF16)
    osq = P.sb("osq", [128, 512], BF16)
    orst = P.sb("orst", [128, 512])
    otmp = P.sb("otmp", [128, 512])
    EPd = B.xt("EP", [128, 8, 2, 129], F32, 1)
    Sctd = B.xt("Sctx", [128, 8, 2, 128], F32, 1)
    Old = B.xt("Oloc", [8, 128, NL], F32, 1)
    Qtd = B.xt("Qt", [8, 2, 128, NL], BF16, 1)
    Sgd = B.xt("Sg", [8, 128, NL], BF16, 1)
    Ycd = B.xt("Yc", [128, 8, NCX], BF16, 1)
    pj = [B.pb[0], B.pb[1]]
    pv_ = Tl(B.pb[2].ap[:, 0:128], "pv")
    pS = [Tl(B.pb[3].ap[:, d * 128:(d + 1) * 128], "pS%d" % d) for d in range(2)]
    pO = [Tl(B.pb[4].ap[:, d * 128:(d + 1) * 128], "pO%d" % d) for d in range(2)]
    pKV = [Tl(B.pb[5].ap[:, d * 128:(d + 1) * 128], "pKV%d" % d) for d in range(2)]
    pT = [Tl(B.pbb.ap[:, d * 128:(d + 1) * 128], "pT%d" % d) for d in range(2)]
    pN = B.pb[6]
    gn = B.pvv("gn")
    segs = [(0, 256), (256, 1280), (1280, 2304)]
    njob = 0
    for h in range(8):
        wb = wbf[h % 2]
        for kc in range(8):
            st = wst[kc % 2]
            P.dma(st[:], hgw[h, :, kc * 640:(kc + 1) * 640], q=("sp" if kc % 2 == 0 else "pool"))
            P.copy("pool" if kc % 2 == 0 else "dve", wb[:, kc].re("p g n -> p (g n)"), st[:])
        def proj(g, c0, c1, pst):
            for kc in range(8):
                P.mm(pst[:, 0:c1 - c0], wb[:, kc, g, :], H1[:, kc, c0:c1], start=(kc == 0), stop=(kc == 7))
        for bi, (c0, c1) in enumerate(BLK5):
            proj(0, c0, c1, pj[0]); P.act(qb[:, c0:c1], pj[0][:, 0:c1 - c0], AF.Silu)
            proj(4, c0, c1, pj[1]); P.act(sg[:, c0:c1], pj[1][:, 0:c1 - c0], AF.Silu)
        for t in range(18):
            for kc in range(8):
                P.mm(pv_[:], H1[:, kc, t * 128:(t + 1) * 128], wb[:, kc, 3, :], start=(kc == 0), stop=(kc == 7))
            P.copy("dve", vtok[:, t, :], pv_[:])
        for d in range(2):
            sgs = segs if d == 0 else segs[::-1]
            first_lat = True
            for (c0, c1) in sgs:
                w = c1 - c0
                nch = w // 64
                sf, lf, kk, bb, ee = [t_[:, 0:w] for t_ in T]
                for (a0, a1) in [(x, min(x + 512, c1)) for x in range(c0, c1, 512)]:
                    pst = pj[njob % 2]; njob += 1
                    proj(1 + d, a0, a1, pst)
                    P.act(sf[:, a0 - c0:a1 - c0], pst[:, 0:a1 - a0], AF.Sigmoid)
                P.act(lf, sf, AF.Ln, bias=lb[:, d, h:h + 1], scale=oml[:, d, h:h + 1])
                P.ts("pool", kk, sf, noml[:, d, h:h + 1], oml[:, d, h:h + 1], ALU.mult, ALU.add)
                b_ = sf
                if d == 0:
                    P.scan("dve", b_, cm[:, 0:w], lf, 0.0, ALU.mult, ALU.add)
                else:
                    P.scan("dve", b_[:, ::-1], cm[:, 0:w], lf[:, ::-1], 0.0, ALU.mult, ALU.add)
                b3 = b_.re("p (n j) -> p n j", j=64)
                if c0 >= NCX:
                    init = 0.0 if first_lat else bgl[d][:, 0:1]
                    if d == 0:
                        P.scan("dve", bb, ones32[:, 0:w], lf, init, ALU.mult, ALU.add)
                        P.copy("pool", bgl[d][:, 0:1], bb[:, w - 1:w])
                    else:
                        P.scan("dve", bb[:, ::-1], ones32[:, 0:w], lf[:, ::-1], init, ALU.mult, ALU.add)
                        P.copy("pool", bgl[d][:, 0:1], bb[:, 0:1])
                    first_lat = False
                    P.act(ee, bb, AF.Exp)
                    P.tt("pool", qt[d][:, c0 - NCX:c1 - NCX], qb[:, c0:c1], ee, ALU.mult)
                d1 = lf
                bref = b3[:, :, 32:33].ap.broadcast_to([128, nch, 64])
                P.tt("dve", d1.re("p (n j) -> p n j", j=64), b3, Vw(b_.t, bref), ALU.subtract)
                P.act(ee, d1, AF.Exp)
                P.tt("pool", qe[d][:, c0:c1], qb[:, c0:c1], ee, ALU.mult)
                P.act(bb, d1, AF.Exp, scale=-1.0)
                P.tt("dve", ke[d][:, c0:c1], kk, bb, ALU.mult)
                P.act(ee, b_, AF.Exp)
                P.tt("pool", qd[d][:, c0:c1], qb[:, c0:c1], ee, ALU.mult)
                lastj = 63 if d == 0 else 0
                P.copy("pool", dec[d][:, c0 // 64:c1 // 64], ee.re("p (n j) -> p n j", j=64)[:, :, lastj])
                blast = b3[:, :, lastj:lastj + 1].ap.broadcast_to([128, nch, 64])
                P.tt("dve", d1.re("p (n j) -> p n j", j=64), Vw(b_.t, blast), b3, ALU.subtract)
                P.act(bb, d1, AF.Exp)
                P.tt("dve", kd[d][:, c0:c1], kk, bb, ALU.mult)
            P.act(EPt[:, d, 128:129], bgl[d][:, 0:1], AF.Exp)
            P.dma(Qtd[h, d], qt[d][:], q="pool")
        P.dma(Sgd[h], sg[:, NCX:NT], q="pool")
        seq_f = [(t, t in (0, 2)) for t in range(18)]
        seq_b = [(t, t in (17, 1)) for t in list(range(17, 1, -1)) + [1, 0]]
        for step in range(18):
            for d in range(2):
                t, fresh = (seq_f if d == 0 else seq_b)[step]
                cs = slice(t * 128, (t + 1) * 128)
                P.tr(pT[d][:], kd[d][:, cs], B.ident_bf[:])
                P.copy("act", kdT[d][:], pT[d][:])
                P.mm(pS[d][:], ke[d][:, cs], qe[d][:, cs])
                P.tt("dve", sT[d][:], pS[d][:], Mk[d][:], ALU.mult)
                for c in ((0, 1) if d == 0 else (1, 0)):
                    ps_ = slice(c * 64, (c + 1) * 64)
                    cc = slice(t * 128 + c * 64, t * 128 + (c + 1) * 64)
                    n = t * 2 + c
                    if not fresh:
                        P.mm(pO[d][:, ps_], Sbf[d][:], qd[d][:, cc], start=True, stop=False)
                    P.mm(pO[d][:, ps_], vtok[ps_, t, :], sT[d][ps_, ps_], start=fresh, stop=True)
                    P.mm(pKV[d][:], kdT[d][ps_, :], vtok[ps_, t, :])
                    if fresh:
                        P.copy("dve", S[d][:], pKV[d][:])
                    else:
                        P.stt("dve", S[d][:], S[d][:], dec[d][:, n:n + 1], pKV[d][:], ALU.mult, ALU.add)
                    P.copy("act", Sbf[d][:], S[d][:])
                    fresh = False
                firstvis = (d == 0) if t < 2 else ((d == 0) == (t <= 8))
                if firstvis:
                    P.copy("dve", oacc[:, cs], pO[d][:])
                else:
                    P.tt("dve", oacc[:, cs], oacc[:, cs], pO[d][:], ALU.add)
                if (d == 0 and t == 1) or (d == 1 and t == 0):
                    P.copy("pool", Sct[:, d, :], S[d][:])
                if (d == 0 and t == 17) or (d == 1 and t == 2):
                    P.copy("pool", EPt[:, d, 0:128], S[d][:])
        P.dma(EPd[:, h], EPt[:])
        P.dma(Sctd[:, h], Sct[:])
        P.dma(Old[h], oacc[:, NCX:NT])
        P.act(osq[:, 0:NCX], oacc[:, 0:NCX], AF.Square)
        P.mm(pN[:, 0:NCX], B.ones_bf[:], osq[:, 0:NCX])
        P.ts("dve", orst[:, 0:NCX], pN[:, 0:NCX], 1.0 / 128.0, EPS, ALU.mult, ALU.add)
        P.act(orst[:, 0:NCX], orst[:, 0:NCX], AF.Sqrt)
        P.recip(orst[:, 0:NCX], orst[:, 0:NCX])
        P.stt("dve", otmp[:, 0:NCX], oacc[:, 0:NCX], gn[:, 0:1], orst[:, 0:NCX], ALU.mult, ALU.mult)
        P.tt("pool", Yc[:, h, :], otmp[:, 0:NCX], sg[:, 0:NCX], ALU.mult)
    P.dma(Ycd[:], Yc[:], final=not B.fused)

def load_mods(B, l, prod):
    P = B.P
    modsd = B.xt("mods%d" % l, [128, 96], F32, prod)
    mods = P.sb("modsL%d" % l, [128, 48, 2])
    P.dma(mods[:].re("p n t -> p (n t)"), modsd[:])
    return mods


def stage2(B, I):
    P = B.P
    Xd = B.xin("XT0", [1024, NT])
    wo = B.xin("hg_wo", [128, 8, 1024])
    EPa = B.xt("EP_all", [8, 128, 8, 2, 129], F32, "x1")
    Sctd = B.xt("Sctx", [128, 8, 2, 128], F32, 1)
    Old = B.xt("Oloc", [8, 128, NL], F32, 1)
    Qtd = B.xt("Qt", [8, 2, 128, NL], BF16, 1)
    Sgd = B.xt("Sg", [8, 128, NL], BF16, 1)
    Ycd = B.xt("Yc", [128, 8, NCX], BF16, 1)
    X1d = B.xt("XT1", [1024, NT], F32, 2)
    HXd = B.xt("HX0", [128, 2, 8], F32, 2)
    mods = load_mods(B, 0, 1)
    Y = P.sb("Y", [128, 8, NT], BF16)
    P.dma(Y[:, :, 0:NCX], Ycd[:])
    msk = B.pvv("msk")
    gn = B.pvv("gn")
    ol = [P.sb("ol%d" % i, [128, NL]) for i in range(2)]
    qt = [[P.sb("qtl%d_%d" % (i, d), [128, NL], BF16) for d in range(2)] for i in range(2)]
    sgl = [P.sb("sgl%d" % i, [128, NL], BF16) for i in range(2)]
    eph = [P.sb("eph%d" % i, [128, 8, 2, 129]) for i in range(2)]
    sct = [P.sb("sctl%d" % i, [128, 2, 128]) for i in range(2)]
    coef = P.sb("coef", [128, 2, 8])
    Sin = [P.sb("Sin%d" % d, [128, 128]) for d in range(2)]
    Sinb = [P.sb("Sinb%d" % d, [128, 128], BF16) for d in range(2)]
    tmpE = P.sb("tmpE", [128, 128])
    osq = P.sb("osq2", [128, 512], BF16)
    orst = P.sb("orst2", [128, 512])
    otmp = P.sb("otmp2", [128, 512])
    pC, pN = B.pb[0], B.pb[1]
    for h in range(8):
        i = h % 2
        P.dma(ol[i][:], Old[h])
        for d in range(2):
            P.dma(qt[i][d][:], Qtd[h, d], q="pool")
        P.dma(sgl[i][:], Sgd[h], q="pool")
        P.dma(eph[i][:], EPa[:, :, h].re("r p d n -> p r d n"))
        P.dma(sct[i][:], Sctd[:, h])
        for d in range(2):
            mo = 0 if d == 0 else 16
            P.tt("dve", coef[:, d, :], eph[i][:, :, d, 128], msk[:, mo:mo + 8], ALU.mult)
            P.tt("dve", coef[:, d, :], coef[:, d, :], msk[:, mo + 8:mo + 16], ALU.add)
            P.copy("dve", Sin[d][:], sct[i][:, d, :])
            for j in (range(8) if d == 0 else range(7, -1, -1)):
                P.ts("pool", tmpE[:], eph[i][:, j, d, 0:128], msk[:, mo + j:mo + j + 1], None, ALU.mult)
                P.stt("dve", Sin[d][:], Sin[d][:], coef[:, d, j:j + 1], tmpE[:], ALU.mult, ALU.add)
            P.copy("act", Sinb[d][:], Sin[d][:])
        for bk in range(4):
            cs = slice(bk * 512, (bk + 1) * 512)
            P.mm(pC[:], Sinb[0][:], qt[i][0][:, cs], start=True, stop=False)
            P.mm(pC[:], Sinb[1][:], qt[i][1][:, cs], start=False, stop=True)
            P.tt("dve", ol[i][:, cs], ol[i][:, cs], pC[:], ALU.add)
            P.act(osq[:], ol[i][:, cs], AF.Square)
            P.mm(pN[:], B.ones_bf[:], osq[:])
            P.ts("dve", orst[:], pN[:], 1.0 / 128.0, EPS, ALU.mult, ALU.add)
            P.act(orst[:], orst[:], AF.Sqrt)
            P.recip(orst[:], orst[:])
            P.stt("dve", otmp[:], ol[i][:, cs], gn[:, 0:1], orst[:], ALU.mult, ALU.mult)
            P.tt("pool", Y[:, h, NCX + bk * 512:NCX + (bk + 1) * 512], otmp[:], sgl[i][:, cs], ALU.mult)
    wob = P.sb("wob", [128, 8, 1024], BF16)
    wst = P.sb("wo_st", [128, 2, 1024])
    for g in range(4):
        P.dma(wst[:], wo[:, g * 2:(g + 1) * 2, :])
        P.copy("pool" if g % 2 else "act", wob[:, g * 2:(g + 1) * 2, :], wst[:])
    xr = [P.sb("xr%d" % i, [128, NT]) for i in range(2)]
    hxs = P.sb("hxs", [128, 2, 8])
    pw = [B.pb[2], B.pb[3]]
    nj = 0
    for n in range(8):
        x = xr[n % 2]
        P.dma(x[:], Xd[n * 128:(n + 1) * 128, :])
        for (c0, c1) in BLK5:
            t = 1 if c0 < NCX else 0
            p_ = pw[nj % 2]; nj += 1
            for kc in range(8):
                P.mm(p_[:, 0:c1 - c0], wob[:, kc, n * 128:(n + 1) * 128], Y[:, kc, c0:c1], start=(kc == 0), stop=(kc == 7))
            P.stt("dve", x[:, c0:c1], p_[:, 0:c1 - c0], mods[:, 16 + n, t:t + 1], x[:, c0:c1], ALU.mult, ALU.add)
        P.dma(X1d[n * 128:(n + 1) * 128, :], x[:], q="pool")
        P.copy("pool", hxs[:, 0, n:n + 1], x[:, NCX:NCX + 1])
        P.copy("pool", hxs[:, 1, n:n + 1], x[:, NT - 1:NT])
    P.dma(HXd[:], hxs[:])

def norm_sb(B, src, w, A, Sh, out, o0):
    P = B.P
    sq = B.nm_sq
    for kc in range(8):
        P.act(sq[:, kc, 0:w], src[:, kc, 0:w], AF.Square)
    ps = B.pb[1]
    for kc in range(8):
        P.mm(ps[:, 0:w], B.ones_bf[:], sq[:, kc, 0:w], start=(kc == 0), stop=(kc == 7))
    rstd = B.nm_rstd
    P.ts("dve", rstd[:, 0:w], ps[:, 0:w], 1.0 / 1024.0, EPS, ALU.mult, ALU.add)
    P.act(rstd[:, 0:w], rstd[:, 0:w], AF.Sqrt)
    P.recip(rstd[:, 0:w], rstd[:, 0:w])
    tmp = B.nm_tmp
    for kc in range(8):
        P.stt("dve", tmp[:, kc % 2, 0:w], src[:, kc, 0:w], A[:, kc:kc + 1], rstd[:, 0:w], ALU.mult, ALU.mult)
        P.act(out[:, kc, o0:o0 + w], tmp[:, kc % 2, 0:w], AF.Identity, bias=Sh[:, kc:kc + 1])


def stage_ffn(B, l, sid, xin_name, xin_prod, hx_name, xout_name, last):
    P = B.P
    Xd = B.xt(xin_name, [1024, NT], F32, xin_prod)
    HXa = B.xt(hx_name + "_all", [8, 128, 2, 8], F32, "x")
    wup = B.xin("wup%d" % l, [22, 128, 2048])
    wdn = B.xin("wdn%d" % l, [128, 22, 1024])
    Ad = B.xt("Aff%d" % l, [22, 128, NT], BF16, sid)
    mods = load_mods(B, l, 1 if l == 0 else 4)
    msk = B.pvv("msk"); edge = B.pvv("edge")
    A2 = P.sb("A2", [128, 8, 2])
    for t in range(2):
        P.stt("dve", A2[:, :, t], mods[:, 32:40, t], 1.0, B.pvv("ng%d1" % l), ALU.add, ALU.mult)
    NW = NT + 4
    CO, LO = 1, 259
    H2 = P.sb("H2", [128, 8, NW], BF16)
    P.memset("pool", H2[:, :, 0:1], 0.0)
    P.memset("pool", H2[:, :, 257:258], 0.0)
    with P.scope():
        norm_bufs(B)
        for (c0, c1) in BLK5:
            t = 1 if c0 < NCX else 0
            o0 = CO + c0 if c0 < NCX else LO + (c0 - NCX)
            norm_mod(B, Xd, c0, c1, A2[:, :, t], mods[:, 24:32, t], H2, o0, B.xst, "h2")
        hxa = P.sb("hxa", [128, 8, 2, 8])
        P.dma(hxa[:], HXa[:].re("r p t k -> p r t k"))
        hx = P.sb("hx", [128, 8, 2])
        tmph = P.sb("tmph", [128, 8])
        for side, (mo, tt_) in enumerate(((32, 1), (40, 0))):
            P.ts("dve", hx[:, :, side], hxa[:, 0, tt_, :], msk[:, mo:mo + 1], None, ALU.mult)
            for j in range(1, 8):
                P.stt("dve", hx[:, :, side], hxa[:, j, tt_, :], msk[:, mo + j:mo + j + 1], hx[:, :, side], ALU.mult, ALU.add)
        hh = P.sb("hh", [128, 8, 2], BF16)
        norm_sb(B, hx, 2, A2[:, :, 0], mods[:, 24:32, 0], hh, 0)
        P.ts("pool", H2[:, :, LO - 1], hh[:, :, 0], edge[:, 0:1], None, ALU.mult)
        P.ts("pool", H2[:, :, LO + NL], hh[:, :, 1], edge[:, 1:2], None, ALU.mult)
    fcw = B.pvv("fcw%d" % l).re("p (k j) -> p k j", k=3)
    fcb = B.pvv("fcb%d" % l)
    blocks = [(0, 512), (512, 1024), (1024, 1536), (1536, 2048), (2048, NW)]
    with P.scope():
        wst = [P.sb("wu_st%d" % i, [128, 2048]) for i in range(2)]
        wbf = [P.sb("wu_bf%d" % i, [128, 8, 2, 128], BF16) for i in range(2)]
        u = [P.sb("u%d" % i, [128, NW]) for i in range(2)]
        cv = [P.sb("cv%d" % i, [128, NW]) for i in range(2)]
        aj = [P.sb("aj%d" % i, [128, NW], BF16) for i in range(2)]
        ctmp = P.sb("ctmp", [128, NW])
        pj = [B.pb[0], B.pb[2], B.pb[3], B.pb[4]]
        nj = 0
        for j in range(22):
            st, wb = wst[j % 2], wbf[j % 2]
            P.dma(st[:], wup[j], q=("sp" if j % 2 == 0 else "pool"))
            P.copy("pool", wb[:].re("p a b c -> p (a b c)"), st[:])
            W_ = NW - 2
            for t in range(2):
                ch = t * 22 + j
                for (c0, c1) in blocks:
                    p_ = pj[nj % 4]; nj += 1
                    for kc in range(8):
                        P.mm(p_[:, 0:c1 - c0], wb[:, kc, t, :], H2[:, kc, c0:c1], start=(kc == 0), stop=(kc == 7))
                    P.copy("act", u[t][:, c0:c1], p_[:, 0:c1 - c0])
                    c1b = min(c1, W_)
                    P.act(cv[t][:, c0 + 1:c1b + 1], p_[:, 0:c1b - c0], AF.Identity, bias=fcb[:, ch:ch + 1], scale=fcw[:, 0, ch:ch + 1])
                P.stt("dve", cv[t][:, 1:1 + W_], u[t][:, 1:1 + W_], fcw[:, 1, ch:ch + 1], cv[t][:, 1:1 + W_], ALU.mult, ALU.add)
                P.stt("dve", cv[t][:, 1:1 + W_], u[t][:, 2:2 + W_], fcw[:, 2, ch:ch + 1], cv[t][:, 1:1 + W_], ALU.mult, ALU.add)
            P.act(cv[0][:, 1:NW - 1], cv[0][:, 1:NW - 1], AF.Silu)
            a_ = aj[j % 2]
            P.tt("dve", a_[:, 1:NW - 1], cv[0][:, 1:NW - 1], cv[1][:, 1:NW - 1], ALU.mult)
            P.dma(Ad[j, :, 0:NCX], a_[:, CO:CO + NCX], q="sp")
            P.dma(Ad[j, :, NCX:NT], a_[:, LO:LO + NL], q="pool")
    Xo = B.xt(xout_name, [1024, NT], F32, sid)
    with P.scope():
        wd = P.sb("wd", [128, 22, 1024], BF16)
        wst = [P.sb("wd_st%d" % i, [128, 2, 1024]) for i in range(2)]
        for g in range(11):
            P.dma(wst[g % 2][:], wdn[:, g * 2:(g + 1) * 2, :], q=("sp" if g % 2 == 0 else "pool"))
            P.copy("pool" if g % 2 else "act", wd[:, g * 2:(g + 1) * 2, :], wst[g % 2][:])
        ab = [P.sb("ab%d" % i, [128, 22, 512], BF16) for i in range(2)]
        xb = [P.sb("xb%d" % i, [128, 512]) for i in range(4)]
        pw = [B.pb[0], B.pb[2], B.pb[3], B.pb[4]]
        if last:
            fsq = P.sb("fsq", [128, 8, 512], BF16)
            xk = P.sb("xk", [128, 8, 512])
            frs = P.sb("frs", [128, 512])
            OUTd = B.xt("OUT", [1024, NL], F32, sid)
            fg = B.pvv("fg")
        else:
            HRo = B.xt("HR", [128, 8, 192], F32, sid)
        nj = 0
        blks = BLK5[1:] if last else BLK5
        for bi, (c0, c1) in enumerate(blks):
            w = c1 - c0
            t = 1 if c0 < NCX else 0
            a_ = ab[bi % 2]
            P.dma(a_[:, :, 0:w], Ad[:, :, c0:c1].re("j p t -> p j t"))
            for n in range(8):
                x = xb[nj % 4]
                p_ = pw[nj % 4]; nj += 1
                P.dma(x[:, 0:w], Xd[n * 128:(n + 1) * 128, c0:c1], q="pool")
                for j in range(22):
                    P.mm(p_[:, 0:w], wd[:, j, n * 128:(n + 1) * 128], a_[:, j, 0:w], start=(j == 0), stop=(j == 21))
                if last:
                    P.stt("dve", xk[:, n, 0:w], p_[:, 0:w], mods[:, 40 + n, t:t + 1], x[:, 0:w], ALU.mult, ALU.add)
                else:
                    P.stt("dve", x[:, 0:w], p_[:, 0:w], mods[:, 40 + n, t:t + 1], x[:, 0:w], ALU.mult, ALU.add)
                    P.dma(Xo[n * 128:(n + 1) * 128, c0:c1], x[:, 0:w], q="sp")
                    if c0 == NCX:
                        P.dma(HRo[:, n, 0:64], x[:, 0:64], q="sp")
                    if c1 == NT:
                        P.dma(HRo[:, n, 64:192], x[:, w - 128:w], q="sp")
            if last:
                for n in range(8):
                    P.act(fsq[:, n, :], xk[:, n, :], AF.Square)
                ps = B.pb[1]
                for n in range(8):
                    P.mm(ps[:], B.ones_bf[:], fsq[:, n, :], start=(n == 0), stop=(n == 7))
                P.ts("dve", frs[:], ps[:], 1.0 / 1024.0, EPS, ALU.mult, ALU.add)
                P.act(frs[:], frs[:], AF.Sqrt)
                P.recip(frs[:], frs[:])
                for n in range(8):
                    P.stt("dve", xk[:, n, :], xk[:, n, :], fg[:, n:n + 1], frs[:], ALU.mult, ALU.mult)
                P.dma(OUTd[:, c0 - NCX:c1 - NCX].re("(n p) t -> p n t", p=128), xk[:])


def stage3(B, I):
    stage_ffn(B, 0, 3, "XT1", 2, "HX0", "XT2", False)


def stage6(B, I):
    stage_ffn(B, 1, 6, "XT3", 5, "HX1", "XT4", True)

def lat_cm(v):
    return v.re("p (r c) -> p c r", c=64)


def stage4(B, I):
    P = B.P
    Xd = B.xt("XT2", [1024, NT], F32, 3)
    HRa = B.xt("HR_all", [8, 128, 8, 192], F32, "x")
    w_ada = B.xin("w_ada", [2, 1024, 6144])
    lwi = B.xin("lru_wi", [8, 128, 8, 2, 128])
    wax = B.xin("lru_wax", [4, 128, 2, 2, 2, 256])
    mods = P.sb("mods1t", [128, 48, 2])
    A = P.sb("modA1", [128, 2, 8, 2])
    with P.scope():
        mods_layer(B, 1, w_ada, mods, A)
    modsd = B.xt("mods1", [128, 96], F32, 4)
    P.dma(modsd[:], mods[:].re("p n t -> p (n t)"))
    msk = B.pvv("msk"); edge = B.pvv("edge")
    H1 = P.sb("H1b", [128, 8, NT], BF16)
    hh = P.sb("hhr", [128, 8, 192], BF16)
    with P.scope():
        norm_bufs(B)
        for (c0, c1) in BLK5:
            t = 1 if c0 < NCX else 0
            norm_mod(B, Xd, c0, c1, A[:, 0, :, t], mods[:, 0:8, t], H1, c0, B.xst, "h1")
        hra = P.sb("hra", [128, 8, 192])
        hsel = P.sb("hsel", [128, 8, 192])
        for j in range(8):
            P.dma(hra[:], HRa[j])
            for (lo, hi, mo) in ((0, 64, 56), (64, 192, 48)):
                if j == 0:
                    P.ts("dve", hsel[:, :, lo:hi], hra[:, :, lo:hi], msk[:, mo:mo + 1], None, ALU.mult)
                else:
                    P.stt("dve", hsel[:, :, lo:hi], hra[:, :, lo:hi], msk[:, mo + j:mo + j + 1], hsel[:, :, lo:hi], ALU.mult, ALU.add)
        norm_sb(B, hsel, 192, A[:, 0, :, 0], mods[:, 0:8, 0], hh, 0)
    lam = B.pvv("lam").re("p (d c) -> p d c", d=2)
    ca = P.sb("ca", [128, 2, 8]); ca2 = P.sb("ca2", [128, 2, 8])
    P.act(ca[:], lam, AF.Exp, scale=-1.0)
    P.act(ca[:], ca[:], AF.Ln, bias=1.0)
    P.ts("dve", ca2[:], ca[:], -16.0, None, ALU.mult)
    P.ts("dve", ca[:], ca[:], -8.0, None, ALU.mult)
    ba = B.pvv("ba").re("p (d c) -> p d c", d=2)
    bx = B.pvv("bx").re("p (d c) -> p d c", d=2)
    cw = B.pvv("cw").re("p (k c) -> p k c", k=4)
    cb = B.pvv("cb")
    segm = P.sb("segm", [128, NL])
    P.memset("pool", segm[:], 1.0)
    P.memset("pool", segm[:].re("p (c r) -> p c r", r=32)[:, :, 0:1], 0.0)
    SMd = B.xt("SM", [128, 8, 2, 2, 64], F32, 4)
    HCd = B.xt("HCs", [128, 8, 2], F32, 4)
    HLd = B.xt("HL", [8, 2, 2, 128, NL], F32, 4)
    Ggd = B.xt("Gg", [8, 128, NL], BF16, 4)
    SMt = P.sb("SMt", [128, 8, 2, 2, 64])
    HCt = P.sb("HCt", [128, 8, 2])
    wst = P.sb("lw_st", [128, 8, 2, 128])
    wbf = [P.sb("lw_bf%d" % i, [128, 8, 2, 128], BF16) for i in range(2)]
    wxst = P.sb("wax_st", [128, 2048])
    wxbf = P.sb("wax_bf", [128, 2, 2, 2, 256], BF16)
    xbuf = P.sb("xbuf", [128, 64, 35])
    cbuf = P.sb("cbuf", [128, 259])
    P.memset("pool", cbuf[:], 0.0)
    xhal = P.sb("xhal", [128, 192])
    xc = [P.sb("xc%d" % i, [128, NT]) for i in range(2)]
    xcb = [P.sb("xcb%d" % i, [128, NT], BF16) for i in range(2)]
    ggb = P.sb("ggb", [128, NL], BF16)
    rr = P.sb("rr", [128, NT]); ii = P.sb("ii", [128, NT]); gt = ii[:, 0:NL]; gtm = rr[:, 0:NL]; aa = P.sb("aa", [128, NT]); uu = P.sb("uu", [128, NT])
    am = P.sb("am", [128, NL]); as_ = P.sb("as_", [128, NL])
    hl = P.sb("hl", [128, NL]); Al = P.sb("Al", [128, NL]); hcx = P.sb("hcx", [128, NCX])
    pj = [B.pb[0], B.pb[2], B.pb[3], B.pb[4]]
    nj = 0
    cblocks = [(i * 16, (i + 1) * 16) for i in range(4)]
    for nb in range(4):
        P.dma(wxst[:], wax[nb].re("p d t k j -> p (d t k j)"))
        P.copy("pool", wxbf[:].re("p d t k j -> p (d t k j)"), wxst[:])
        for sub in range(2):
            cc = nb * 2 + sub
            wb = wbf[cc % 2]
            P.dma(wst[:], lwi[cc])
            P.copy("pool", wb[:], wst[:])
            for (ca_, cb_) in cblocks:
                p_ = pj[nj % 4]; nj += 1
                for kc in range(8):
                    P.mm(p_[:], wb[:, kc, 0, :], lat_cm(H1[:, kc, NCX:NT])[:, ca_:cb_, :], start=(kc == 0), stop=(kc == 7))
                P.copy("act", gt[:, ca_ * 32:cb_ * 32], p_[:])
                p_ = pj[nj % 4]; nj += 1
                for kc in range(8):
                    P.mm(p_[:], wb[:, kc, 1, :], lat_cm(H1[:, kc, NCX:NT])[:, ca_:cb_, :], start=(kc == 0), stop=(kc == 7))
                P.copy("act", xbuf[:, ca_:cb_, 2:34], p_[:].re("p (c r) -> p c r", r=32))
            p_ = pj[nj % 4]; nj += 1
            for kc in range(8):
                P.mm(p_[:, 0:256], wb[:, kc, 1, :], H1[:, kc, 0:NCX], start=(kc == 0), stop=(kc == 7))
            P.copy("act", cbuf[:, 2:258], p_[:, 0:256])
            p_ = pj[nj % 4]; nj += 1
            for kc in range(8):
                P.mm(p_[:, 0:192], wb[:, kc, 1, :], hh[:, kc, :], start=(kc == 0), stop=(kc == 7))
            P.copy("act", xhal[:], p_[:, 0:192])
            for (slot, lo) in ((0, 64), (1, 128)):
                P.ts("dve", xbuf[:, :, slot], xhal[:, lo:lo + 64], edge[:, 4:5], None, ALU.mult)
                P.stt("dve", xbuf[:, 1:64, slot], xhal[:, lo:lo + 63], edge[:, 2:3], xbuf[:, 1:64, slot], ALU.mult, ALU.add)
            P.ts("dve", xbuf[:, :, 34], xhal[:, 0:64], edge[:, 5:6], None, ALU.mult)
            P.stt("dve", xbuf[:, 0:63, 34], xhal[:, 1:64], edge[:, 3:4], xbuf[:, 0:63, 34], ALU.mult, ALU.add)
            xl = xc[sub][:, NCX:NT].re("p (c r) -> p c r", r=32)
            P.ts("dve", xl, xbuf[:, :, 0:32], cw[:, 0, cc:cc + 1], cb[:, cc:cc + 1], ALU.mult, ALU.add)
            for k_ in range(1, 4):
                P.stt("dve", xl, xbuf[:, :, k_:k_ + 32], cw[:, k_, cc:cc + 1], xl, ALU.mult, ALU.add)
            xcc = xc[sub][:, 0:NCX]
            P.ts("dve", xcc, cbuf[:, 0:256], cw[:, 0, cc:cc + 1], cb[:, cc:cc + 1], ALU.mult, ALU.add)
            for k_ in range(1, 4):
                P.stt("dve", xcc, cbuf[:, k_:k_ + 256], cw[:, k_, cc:cc + 1], xcc, ALU.mult, ALU.add)
            P.copy("pool", xcb[sub][:], xc[sub][:])
            P.act(gtm, gt, AF.Square)
            P.ts("pool", gtm, gtm, 0.044715, 1.0, ALU.mult, ALU.add)
            P.tt("pool", gtm, gtm, gt, ALU.mult)
            P.act(gtm, gtm, AF.Sigmoid, scale=1.5957691216057308)
            P.tt("pool", ggb[:], gtm, gt, ALU.mult)
            P.dma(Ggd[cc], ggb[:], q="pool")
        for d in range(2):
            for js in range(2):
                cc = nb * 2 + js
                for ty, dst, bias in ((0, rr, ba), (1, ii, bx)):
                    for (c0, c1) in BLK5:
                        p_ = pj[nj % 4]; nj += 1
                        for ks in range(2):
                            P.mm(p_[:, 0:c1 - c0], wxbf[:, d, ty, ks, js * 128:(js + 1) * 128], xcb[ks][:, c0:c1], start=(ks == 0), stop=(ks == 1))
                        P.act(dst[:, c0:c1], p_[:, 0:c1 - c0], AF.Sigmoid, bias=bias[:, d, cc:cc + 1])
                P.act(aa[:], rr[:], AF.Exp, scale=ca[:, d, cc:cc + 1])
                P.act(rr[:], rr[:], AF.Exp, scale=ca2[:, d, cc:cc + 1])
                P.act(rr[:], rr[:], AF.Sqrt, bias=1.0, scale=-1.0)
                P.tt("pool", uu[:], ii[:], xc[js][:], ALU.mult)
                P.tt("pool", uu[:], uu[:], rr[:], ALU.mult)
                al, ul = aa[:, NCX:NT], uu[:, NCX:NT]
                if d == 0:
                    P.scan("dve", hcx[:], aa[:, 0:NCX], uu[:, 0:NCX], 0.0, ALU.mult, ALU.add)
                    P.copy("pool", HCt[:, cc, 0:1], hcx[:, NCX - 1:NCX])
                    P.tt("pool", am[:], al, segm[:], ALU.mult)
                    P.tt("pool", as_[:], al, am[:], ALU.subtract)
                    P.scan("dve", hl[:], am[:], ul, 0.0, ALU.mult, ALU.add)
                    P.scan("dve", Al[:], am[:], as_[:], 0.0, ALU.mult, ALU.add)
                    e_ = 31
                else:
                    P.scan("dve", hcx[:, ::-1], aa[:, 0:NCX][:, ::-1], uu[:, 0:NCX][:, ::-1], 0.0, ALU.mult, ALU.add)
                    P.copy("pool", HCt[:, cc, 1:2], hcx[:, 0:1])
                    P.tt("pool", am[:, ::-1], al[:, ::-1], segm[:], ALU.mult)
                    P.tt("pool", as_[:], al, am[:], ALU.subtract)
                    P.scan("dve", hl[:, ::-1], am[:, ::-1], ul[:, ::-1], 0.0, ALU.mult, ALU.add)
                    P.scan("dve", Al[:, ::-1], am[:, ::-1], as_[:, ::-1], 0.0, ALU.mult, ALU.add)
                    e_ = 0
                P.copy("pool", SMt[:, cc, d, 0, :], Al[:].re("p (c r) -> p c r", r=32)[:, :, e_])
                P.copy("pool", SMt[:, cc, d, 1, :], hl[:].re("p (c r) -> p c r", r=32)[:, :, e_])
                P.dma(HLd[cc, d, 0], hl[:], q="sp")
                P.dma(HLd[cc, d, 1], Al[:], q="pool")
    P.dma(SMd[:], SMt[:])
    P.dma(HCd[:], HCt[:])


def stage5(B, I):
    P = B.P
    Xd = B.xt("XT2", [1024, NT], F32, 3)
    SMa = B.xt("SM_all", [8, 128, 8, 2, 2, 64], F32, "x")
    HCd = B.xt("HCs", [128, 8, 2], F32, 4)
    HLd = B.xt("HL", [8, 2, 2, 128, NL], F32, 4)
    Ggd = B.xt("Gg", [8, 128, NL], BF16, 4)
    wo = B.xin("lru_wo", [128, 8, 1024])
    X3d = B.xt("XT3", [1024, NT], F32, 5)
    HXd = B.xt("HX1", [128, 2, 8], F32, 5)
    mods = load_mods(B, 1, 4)
    msk = B.pvv("msk"); k1h = B.pvv("k1h")
    HCt = P.sb("HCt5", [128, 8, 2])
    P.dma(HCt[:], HCd[:])
    Yl = P.sb("Yl", [128, 8, NL], BF16)
    sma = P.sb("sma", [128, 8, 2, 2, 64])
    seq = P.sb("seq", [128, 2, 2, 64, 4])
    G = P.sb("G", [128, 258])
    Hin = [P.sb("Hin%d" % d, [128, 64]) for d in range(2)]
    hl = [P.sb("hl5_%d" % i, [128, NL]) for i in range(2)]
    Al = [P.sb("Al5_%d" % i, [128, NL]) for i in range(2)]
    gg = P.sb("gg5", [128, NL], BF16)
    ysum = P.sb("ysum", [128, NL])
    for cc in range(8):
        P.dma(sma[:], SMa[:, :, cc].re("r p d t c -> p r d t c"))
        P.dma(gg[:], Ggd[cc], q="pool")
        for kq in range(4):
            P.ts("dve", seq[:, :, :, :, kq], sma[:, kq], msk[:, 72 + kq:73 + kq], None, ALU.mult)
            P.stt("dve", seq[:, :, :, :, kq], sma[:, 4 + kq], msk[:, 76 + kq:77 + kq], seq[:, :, :, :, kq], ALU.mult, ALU.add)
        for d in range(2):
            P.dma(hl[d][:], HLd[cc, d, 0], q="sp")
            P.dma(Al[d][:], HLd[cc, d, 1], q="pool")
            Pq = seq[:, d, 0].re("p c k -> p (c k)")
            Eq = seq[:, d, 1].re("p c k -> p (c k)")
            if d == 0:
                P.copy("pool", G[:, 0:1], HCt[:, cc, 0:1])
                P.scan("dve", G[:, 1:257], Pq, Eq, HCt[:, cc, 0:1], ALU.mult, ALU.add)
                Gs = G[:, 0:256].re("p (c k) -> p c k", k=4)
            else:
                P.copy("pool", G[:, 257:258], HCt[:, cc, 1:2])
                P.scan("dve", G[:, 1:257][:, ::-1], Pq[:, ::-1], Eq[:, ::-1], HCt[:, cc, 1:2], ALU.mult, ALU.add)
                Gs = G[:, 2:258].re("p (c k) -> p c k", k=4)
            P.ts("dve", Hin[d][:], Gs[:, :, 0], k1h[:, 0:1], None, ALU.mult)
            for kq in range(1, 4):
                P.stt("dve", Hin[d][:], Gs[:, :, kq], k1h[:, kq:kq + 1], Hin[d][:], ALU.mult, ALU.add)
            hb = Hin[d][:].ap.unsqueeze(2).broadcast_to([128, 64, 32])
            A3 = Al[d][:].re("p (c r) -> p c r", r=32)
            P.tt("pool", A3, A3, Vw(Hin[d], hb), ALU.mult)
            P.tt("pool", hl[d][:], hl[d][:], Al[d][:], ALU.add)
        P.tt("dve", ysum[:], hl[0][:], hl[1][:], ALU.add)
        P.tt("pool", Yl[:, cc, :], ysum[:], gg[:], ALU.mult)
    wob = P.sb("lwob", [128, 8, 1024], BF16)
    wst = P.sb("lwo_st", [128, 2, 1024])
    for g in range(4):
        P.dma(wst[:], wo[:, g * 2:(g + 1) * 2, :])
        P.copy("pool" if g % 2 else "act", wob[:, g * 2:(g + 1) * 2, :], wst[:])
    xr = [P.sb("xr5_%d" % i, [128, NT]) for i in range(2)]
    hxs = P.sb("hxs5", [128, 2, 8])
    pw = [B.pb[2], B.pb[3]]
    nj = 0
    for n in range(8):
        x = xr[n % 2]
        P.dma(x[:], Xd[n * 128:(n + 1) * 128, :])
        xl = lat_cm(x[:, NCX:NT])
        for bi in range(4):
            p_ = pw[nj % 2]; nj += 1
            for kc in range(8):
                P.mm(p_[:], wob[:, kc, n * 128:(n + 1) * 128], Yl[:, kc, bi * 512:(bi + 1) * 512], start=(kc == 0), stop=(kc == 7))
            P.stt("dve", xl[:, bi * 16:(bi + 1) * 16, :], p_[:].re("p (c r) -> p c r", r=32), mods[:, 16 + n, 0:1], xl[:, bi * 16:(bi + 1) * 16, :], ALU.mult, ALU.add)
        P.dma(X3d[n * 128:(n + 1) * 128, :], x[:], q="pool")
        P.copy("pool", hxs[:, 0, n:n + 1], x[:, NCX:NCX + 1])
        P.copy("pool", hxs[:, 1, n:n + 1], x[:, NT - 1:NT])
    P.dma(HXd[:], hxs[:])

STAGE_FNS = {}


def host_layouts(inputs):
    I = {k: np.asarray(v) for k, v in inputs.items()}
    shared = {}
    shared["w_ada"] = np.ascontiguousarray(I["w_ada"], dtype=np.float32)
    w = I["hg_w_in"][0].reshape(8, 128, 5, 8, 128)
    shared["hg_w"] = np.ascontiguousarray(w.transpose(3, 1, 0, 2, 4).reshape(8, 128, 8 * 5 * 128))
    shared["hg_wo"] = np.ascontiguousarray(I["hg_w_out"][0].reshape(8, 128, 1024).transpose(1, 0, 2))
    for l in range(2):
        wu = I["ffn_w_up"][l].reshape(8, 128, 2, 22, 128)
        shared["wup%d" % l] = np.ascontiguousarray(wu.transpose(3, 1, 0, 2, 4).reshape(22, 128, 2048))
        shared["wdn%d" % l] = np.ascontiguousarray(I["ffn_w_down"][l].reshape(22, 128, 1024).transpose(1, 0, 2))
    li = I["lru_w_in"][0].reshape(8, 128, 2, 8, 128)
    shared["lru_wi"] = np.ascontiguousarray(li.transpose(3, 1, 0, 2, 4))
    wa = I["lru_wa"][0].reshape(2, 4, 2, 128, 256)
    wx = I["lru_wx"][0].reshape(2, 4, 2, 128, 256)
    wax = np.stack([wa, wx], axis=0)
    shared["lru_wax"] = np.ascontiguousarray(wax.transpose(2, 4, 1, 0, 3, 5))
    shared["lru_wo"] = np.ascontiguousarray(I["lru_w_out"][0].reshape(8, 128, 1024).transpose(1, 0, 2))
    percore = []
    pvs = None
    for core in range(8):
        b, k = core // 4, core % 4
        d = {}
        d["XT0"] = np.ascontiguousarray(np.concatenate([I["ctx"][b].T, I["x"][b, k * NL:(k + 1) * NL].T], axis=1))
        pv = pv_layout(I, core)
        d["pvec_in"] = pv.pack()
        pvs = pv
        percore.append(d)
    return I, shared, percore, pvs


def run_stage(stage_ids, fn_list, store, shared, pvs, fused=False):
    B = Bld(stage_ids, fused, pvs.off, pvs.n)
    setup_common(B)
    for fn in fn_list:
        fn(B, None)
    B.P.emit()
    in_maps = []
    for c in range(8):
        m = {}
        for n in B.ins:
            m[n] = shared[n] if n in shared else store[c][n]
        in_maps.append(m)
    res = run_bass_kernel_spmd(B.nc, in_maps, core_ids=list(range(8)))
    for c in range(8):
        for n in B.outs:
            store[c][n] = np.asarray(res.results[c][n])
    return B


def gather(store, name):
    g = np.ascontiguousarray(np.stack([store[c][name] for c in range(8)], axis=0))
    for c in range(8):
        store[c][name + "_all"] = g


def kernel_unfused(**inputs):
    I, shared, store, pvs = host_layouts(inputs)
    run_stage([1], [stage1], store, shared, pvs)
    gather(store, "EP")
    run_stage([2], [stage2], store, shared, pvs)
    gather(store, "HX0")
    run_stage([3], [stage3], store, shared, pvs)
    gather(store, "HR")
    run_stage([4], [stage4], store, shared, pvs)
    gather(store, "SM")
    run_stage([5], [stage5], store, shared, pvs)
    gather(store, "HX1")
    run_stage([6], [stage6], store, shared, pvs)
    out = np.empty((2, 8192, 1024), np.float32)
    for c in range(8):
        b, k = c // 4, c % 4
        out[b, k * NL:(k + 1) * NL, :] = store[c]["OUT"].T
    return out


EXCH = {"EP": [128, 8 * 2 * 129], "HX0": [128, 16], "HR": [128, 8 * 192], "SM": [128, 8 * 2 * 2 * 64], "HX1": [128, 16]}
EXCH_FULL = {"EP": ([128, 8, 2, 129], [8, 128, 8, 2, 129]), "HX0": ([128, 2, 8], [8, 128, 2, 8]),
             "HR": ([128, 8, 192], [8, 128, 8, 192]), "SM": ([128, 8, 2, 2, 64], [8, 128, 8, 2, 2, 64]),
             "HX1": ([128, 2, 8], [8, 128, 2, 8])}


def build_fused(pvs):
    B = Bld([1, 2, 3, 4, 5, 6], True, pvs.off, pvs.n)
    P = B.P
    setup_common(B)
    B.xts["OUT"] = P.dram("OUT", [1024, NL], F32, kind="ExternalOutput")
    B.outs.append("OUT")

    def exch(name):
        shp, shp_all = EXCH_FULL[name]
        src = B.xt(name, shp, F32, 0)
        dst = B.xt(name + "_all", shp_all, F32, 0)
        n = 1
        for d_ in shp[1:]:
            n *= d_
        P.allgather(Vw(dst, _flat2(dst.ap, 8 * 128, n)), Vw(src, _flat2(src.ap, 128, n)))

    seq = [(stage1, "EP"), (stage2, "HX0"), (stage3, "HR"), (stage4, "SM"), (stage5, "HX1"), (stage6, None)]
    for fn, ex in seq:
        with P.scope():
            fn(B, None)
        if ex is not None:
            exch(ex)
    P.emit()
    return B


def _flat2(ap, rows, n):
    nd = len(ap.shape)
    names = " ".join("a%d" % i for i in range(nd))
    if ap.shape[0] == rows:
        return ap.rearrange("%s -> a0 (%s)" % (names, " ".join("a%d" % i for i in range(1, nd))))
    return ap.rearrange("%s -> (a0 a1) (%s)" % (names, " ".join("a%d" % i for i in range(2, nd))))


def kernel(**inputs):
    I, shared, store, pvs = host_layouts(inputs)
    B = build_fused(pvs)
    in_maps = []
    for c in range(8):
        m = {}
        for n in B.ins:
            m[n] = shared[n] if n in shared else store[c][n]
        in_maps.append(m)
    res = run_bass_kernel_spmd(B.nc, in_maps, core_ids=list(range(8)))
    out = np.empty((2, 8192, 1024), np.float32)
    for c in range(8):
        b, k = c // 4, c % 4
        out[b, k * NL:(k + 1) * NL, :] = np.asarray(res.results[c]["OUT"]).T
    return out


STAGES = {1: (None, "EP"), 2: (None, "HX0"), 3: (None, "HR"), 4: (None, "SM"), 5: (None, "HX1"), 6: (None, None)}


def run_group(ids, store, shared, pvs):
    fns = {1: stage1, 2: stage2, 3: stage3, 4: stage4, 5: stage5, 6: stage6}
    B = Bld(ids, False, pvs.off, pvs.n)
    P = B.P
    internal = set()
    for i in ids[:-1]:
        ex = STAGES[i][1]
        internal.add(ex); internal.add(ex + "_all")
    B.internal = internal
    setup_common(B)
    for i in ids:
        with P.scope():
            fns[i](B, None)
        ex = STAGES[i][1]
        if ex is not None and i != ids[-1]:
            shp, shp_all = EXCH_FULL[ex]
            src = B.xt(ex, shp, F32, 0)
            dst = B.xt(ex + "_all", shp_all, F32, 0)
            n = 1
            for d_ in shp[1:]:
                n *= d_
            P.allgather(Vw(dst, _flat2(dst.ap, 8 * 128, n)), Vw(src, _flat2(src.ap, 128, n)))
    P.emit()
    in_maps = []
    for c in range(8):
        m = {}
        for n in B.ins:
            m[n] = shared[n] if n in shared else store[c][n]
        in_maps.append(m)
    res = run_bass_kernel_spmd(B.nc, in_maps, core_ids=list(range(8)))
    for c in range(8):
        for n in B.outs:
            store[c][n] = np.asarray(res.results[c][n])
    ex = STAGES[ids[-1]][1]
    if ex is not None:
        gather(store, ex)


def kernel_groups(groups, **inputs):
    I, shared, store, pvs = host_layouts(inputs)
    for g in groups:
        run_group(g, store, shared, pvs)
    out = np.empty((2, 8192, 1024), np.float32)
    for c in range(8):
        b, k = c // 4, c % 4
        out[b, k * NL:(k + 1) * NL, :] = store[c]["OUT"].T
    return out
```

```python
import numpy as np
from contextlib import ExitStack
import concourse.bass as bass
import concourse.mybir as mybir
from concourse.bass_utils import run_bass_kernel_spmd

F32 = mybir.dt.float32
BF16 = mybir.dt.bfloat16
AF = mybir.ActivationFunctionType
ALU = mybir.AluOpType
AX = mybir.AxisListType


class Tl:
    __slots__ = ("ap", "w", "r", "name")

    def __init__(self, ap, name=""):
        self.ap = ap
        self.w = {}
        self.r = {}
        self.name = name

    def __getitem__(self, idx):
        return Vw(self, self.ap[idx])

    def v(self, ap):
        return Vw(self, ap)


class Vw:
    __slots__ = ("t", "ap")

    def __init__(self, t, ap):
        self.t = t
        self.ap = ap

    def __getitem__(self, idx):
        return Vw(self.t, self.ap[idx])

    def re(self, pat, **kw):
        return Vw(self.t, self.ap.rearrange(pat, **kw))


def _ap(x):
    return x.ap if isinstance(x, Vw) else x


class Prog:
    ENGS = ("pe", "act", "dve", "pool", "sp")

    def __init__(self, nc):
        self.nc = nc
        self.es = ExitStack()
        self.es_cur = self.es
        self.q = {e: [] for e in self.ENGS}
        self.cnt = {}
        self.sems = {}
        self.waited = {}
        self.ring = {"sp": 12, "pool": 6, "act": 6}
        self.ringpos = {k: 0 for k in self.ring}
        self.final = []
        for e in ("pe", "act", "dve", "pool"):
            self.sems[e] = self.es.enter_context(nc.semaphore("s_" + e))
            self.cnt[e] = 0
        for qn, k in self.ring.items():
            for j in range(k):
                key = ("d", qn, j)
                self.sems[key] = self.es.enter_context(nc.semaphore("d_%s%d" % (qn, j)))
                self.cnt[key] = 0
        self.nops = 0
        self.cc_inc = 1
        self.marks = {e: set() for e in ("pe", "act", "dve", "pool")}

    def barrier(self):
        cur = dict(self.cnt)
        for e in self.ENGS:
            waits = []
            for k, n in cur.items():
                if n and not (k == "pe" and e == "pe") and self.waited.get((e, k), 0) < n:
                    self.waited[(e, k)] = n
                    waits.append((k, n))
                    if k in self.marks:
                        self.marks[k].add(n)
            self.q[e].append(("wait", waits))

    def scope(self):
        prog = self

        class _S:
            def __enter__(self_):
                self_.old = prog.es_cur
                prog.es_cur = ExitStack()
                return self_

            def __exit__(self_, *a):
                prog.barrier()
                prog.es_cur.close()
                prog.es_cur = self_.old
                return False

        return _S()

    def sb(self, name, shape, dt=F32):
        self.nalloc = getattr(self, "nalloc", 0) + 1
        t = self.es_cur.enter_context(self.nc.sbuf_tensor("sb%d_%s" % (self.nalloc, name), list(shape), dt))
        return Tl(t[:], name)

    def ps(self, name, shape, dt=F32):
        t = self.es.enter_context(self.nc.psum_tensor("ps_" + name, list(shape), dt))
        return Tl(t[:], name)

    def dram(self, name, shape, dt=F32, kind="Internal"):
        t = self.nc.dram_tensor(name, list(shape), dt, kind=kind)
        return Tl(t.ap(), ("D:" if kind == "ExternalOutput" else "d:") + name)

    def _op(self, eng, fn, reads, writes, dma=False, dinc=16, key=None):
        deps = {}
        for v in reads:
            t = v.t
            for k, n in t.w.items():
                if deps.get(k, 0) < n:
                    deps[k] = n
        for v in writes:
            t = v.t
            for k, n in t.w.items():
                if deps.get(k, 0) < n:
                    deps[k] = n
            for k, n in t.r.items():
                if deps.get(k, 0) < n:
                    deps[k] = n
        if key is not None:
            prev = self.cnt[key]
            self.cnt[key] = prev + dinc
            inc = dinc
        elif dma:
            j = self.ringpos[eng]
            self.ringpos[eng] = (j + 1) % self.ring[eng]
            key = ("d", eng, j)
            prev = self.cnt[key]
            if prev and deps.get(key, 0) < prev:
                deps[key] = prev
            self.cnt[key] = prev + dinc
            inc = dinc
        else:
            key = eng
            self.cnt[key] += 1
            inc = 1
        tok = self.cnt[key]
        waits = []
        for k, n in deps.items():
            if k == "pe" and eng == "pe":
                continue
            if self.waited.get((eng, k), 0) >= n:
                continue
            self.waited[(eng, k)] = n
            waits.append((k, n))
            if k in self.marks:
                self.marks[k].add(n)
        self.q[eng].append(("op", waits, fn, key, tok, inc))
        self.nops += 1
        for v in reads:
            v.t.r[key] = tok
        for v in writes:
            v.t.w[key] = tok
        return (key, tok)

    def mm(self, out, lhsT, rhs, start=True, stop=True):
        o, l, r = out.ap, lhsT.ap, rhs.ap
        return self._op("pe", lambda e: e.matmul(o, l, r, start=start, stop=stop), [lhsT, rhs], [out])

    def tr(self, out, in_, ident):
        o, i, d = out.ap, in_.ap, ident.ap
        return self._op("pe", lambda e: e.transpose(o, i, d), [in_, ident], [out])

    def act(self, out, in_, func, bias=0.0, scale=1.0, accum=None, eng="act"):
        rd = [in_] + [x for x in (bias, scale) if isinstance(x, Vw)]
        wr = [out] + ([accum] if accum is not None else [])
        o, i, b, s = out.ap, in_.ap, _ap(bias), _ap(scale)
        if accum is not None:
            a = accum.ap
            return self._op(eng, lambda e: e.activation(o, i, func, bias=b, scale=s, accum_out=a), rd, wr)
        return self._op(eng, lambda e: e.activation(o, i, func, bias=b, scale=s), rd, wr)

    def tt(self, eng, out, in0, in1, op):
        o, a, b = out.ap, in0.ap, in1.ap
        return self._op(eng, lambda e: e.tensor_tensor(o, a, b, op), [in0, in1], [out])

    def ts(self, eng, out, in0, s1, s2=None, op0=ALU.mult, op1=None, accum=None):
        rd = [in0] + [x for x in (s1, s2) if isinstance(x, Vw)]
        wr = [out] + ([accum] if accum is not None else [])
        o, a, x1, x2 = out.ap, in0.ap, _ap(s1), _ap(s2)
        kw = {}
        if op1 is not None:
            kw["op1"] = op1
        if accum is not None:
            kw["accum_out"] = accum.ap
        return self._op(eng, lambda e: e.tensor_scalar(o, a, x1, x2, op0, **kw), rd, wr)

    def stt(self, eng, out, in0, scalar, in1, op0, op1):
        rd = [in0, in1] + ([scalar] if isinstance(scalar, Vw) else [])
        o, a, s, b = out.ap, in0.ap, _ap(scalar), in1.ap
        return self._op(eng, lambda e: e.scalar_tensor_tensor(o, a, s, b, op0, op1), rd, [out])

    def scan(self, eng, out, d0, d1, init, op0, op1):
        rd = [d0, d1] + ([init] if isinstance(init, Vw) else [])
        o, a, b, i = out.ap, d0.ap, d1.ap, _ap(init)
        return self._op(eng, lambda e: e.tensor_tensor_scan(o, a, b, i, op0, op1), rd, [out])

    def copy(self, eng, out, in_):
        o, i = out.ap, in_.ap
        if eng == "act":
            return self._op(eng, lambda e: e.copy(o, i), [in_], [out])
        return self._op(eng, lambda e: e.tensor_copy(o, i), [in_], [out])

    def memset(self, eng, out, val):
        o = out.ap
        return self._op(eng, lambda e: e.memset(o, val), [], [out])

    def recip(self, out, in_):
        o, i = out.ap, in_.ap
        return self._op("dve", lambda e: e.reciprocal(o, i), [in_], [out])

    def reduce(self, eng, out, in_, op=ALU.add, axis=AX.X):
        o, i = out.ap, in_.ap
        return self._op(eng, lambda e: e.tensor_reduce(o, i, axis, op), [in_], [out])

    def iota(self, out, pattern, base=0, cm=0):
        o = out.ap
        return self._op("pool", lambda e: e.iota(o, pattern, base=base, channel_multiplier=cm), [], [out])

    def affine_select(self, out, in_, pattern, cmp, fill, base=0, cm=0):
        o, i = out.ap, in_.ap
        return self._op("pool", lambda e: e.affine_select(o, i, pattern, cmp, fill, base=base, channel_multiplier=cm), [in_], [out])

    def dma(self, out, in_, q="sp", final=False, **kw):
        o, i = out.ap, in_.ap
        tok = self._op(q, lambda e: e.dma_start(o, i, **kw), [in_], [out], dma=True)
        if final or getattr(out.t, "name", "").startswith("D:"):
            self.final.append(tok)
        return tok

    def allgather(self, out, in_, ncores=8):
        o, i = out.ap, in_.ap
        groups = [list(range(ncores))]
        self.barrier()
        idx = getattr(self, "ncc", 0)
        self.ncc = idx + 1
        key = ("cc", idx)
        self.sems[key] = self.es.enter_context(self.nc.semaphore("cc_%d" % idx))
        self.cnt[key] = 0
        tok = self._op("pool", lambda e: e.collective_compute("AllGather", ALU.bypass, replica_groups=groups, ins=[i.opt()], outs=[o.opt()]), [in_], [out], dma=True, dinc=1, key=key)
        self.barrier()
        return tok

    def emit(self):
        nc = self.nc
        fin = list(self.final)
        for k, n in fin:
            if k in self.marks:
                self.marks[k].add(n)
        self.q["sp"].append(("wait", fin))
        sems = self.sems
        import bisect
        ranks = {k: sorted(v) for k, v in self.marks.items()}

        def tr(k, n):
            if k in ranks:
                return bisect.bisect_right(ranks[k], n)
            return n

        q = self.q
        marks = self.marks

        def play(e, items):
            for it in items:
                if it[0] == "wait":
                    for k, n in it[1]:
                        e.wait_ge(sems[k], tr(k, n))
                else:
                    _, waits, fn, key, tok, inc = it
                    for k, n in waits:
                        e.wait_ge(sems[k], tr(k, n))
                    ins = fn(e)
                    if key in marks:
                        if tok in marks[key]:
                            ins.then_inc(sems[key], 1)
                    else:
                        ins.then_inc(sems[key], inc)

        with nc.Block() as block:
            @block.tensor
            def _(e):
                play(e, q["pe"])

            @block.scalar
            def _(e):
                play(e, q["act"])

            @block.vector
            def _(e):
                play(e, q["dve"])

            @block.gpsimd
            def _(e):
                play(e, q["pool"])

            @block.sync
            def _(e):
                play(e, q["sp"])
        self.es.close()


NT, NCX, NL = 2304, 256, 2048
EPS = 1e-6
BLK5 = [(0, 256), (256, 768), (768, 1280), (1280, 1792), (1792, 2304)]


class PV:
    def __init__(self):
        self.off = {}
        self.cols = []
        self.n = 0

    def add(self, name, arr):
        arr = np.ascontiguousarray(arr, dtype=np.float32).reshape(128, -1)
        self.off[name] = (self.n, arr.shape[1])
        self.cols.append(arr)
        self.n += arr.shape[1]

    def pack(self):
        return np.ascontiguousarray(np.concatenate(self.cols, axis=1))


def fm(v):
    v = np.asarray(v)
    lead = v.shape[:-1]
    C = v.shape[-1] // 128
    v = v.reshape(lead + (C, 128))
    return np.moveaxis(v, -1, 0)


def pv_layout(inputs, core):
    b, k = core // 4, core % 4
    pv = PV()
    I = inputs
    for l in range(2):
        for i in range(2):
            pv.add("ng%d%d" % (l, i), fm(I["norm_g"][l, i]))
    pv.add("fg", fm(I["final_g"]))
    for l in range(2):
        pv.add("bada%d" % l, fm(I["b_ada"][l]))
    pv.add("c", np.stack([fm(I["c"][b]), fm(I["c_ctx"])], axis=-1))
    pv.add("lbl", fm(I["hg_lb_logits"]))
    pv.add("gn", np.asarray(I["hg_gnorm"][0]).reshape(128, 1))
    pv.add("cw", fm(I["lru_conv_w"][0]))
    pv.add("cb", fm(I["lru_conv_b"][0]))
    pv.add("ba", fm(I["lru_ba"][0]))
    pv.add("bx", fm(I["lru_bx"][0]))
    pv.add("lam", fm(I["lru_lambda"][0]))
    for l in range(2):
        pv.add("fcw%d" % l, fm(I["ffn_conv_w"][l]))
        pv.add("fcb%d" % l, fm(I["ffn_conv_b"][l]))
    ranks = np.arange(8)
    same = (ranks // 4) == b
    rk = ranks % 4
    m = np.zeros((10, 8), np.float32)
    m[0] = same & (rk < k)
    m[1] = 1.0 - m[0]
    m[2] = same & (rk > k)
    m[3] = 1.0 - m[2]
    m[4] = same & (rk == k - 1)
    m[5] = same & (rk == k + 1)
    m[6] = same & (rk == (k - 1) % 4)
    m[7] = same & (rk == (k + 1) % 4)
    m[8] = same & (rk == k)
    m[9] = same
    pv.add("msk", np.broadcast_to(m.reshape(1, 80), (128, 80)))
    e = np.array([k > 0, k < 3, k == 0, k == 3, k != 0, k != 3], np.float32)
    pv.add("edge", np.broadcast_to(e.reshape(1, 6), (128, 6)))
    kk = np.zeros(4, np.float32); kk[k] = 1.0
    pv.add("k1h", np.broadcast_to(kk.reshape(1, 4), (128, 4)))
    return pv


class Bld:
    def __init__(self, stages, fused, pvoff, nv):
        self.stages = set(stages)
        self.fused = fused
        self.nc = bass.Bass("TRN2", target_bir_lowering=False)
        self.P = Prog(self.nc)
        self.ins = []
        self.outs = []
        self.pvoff = pvoff
        self.nv = nv
        self.xts = {}

    def xin(self, name, shape, dt=F32):
        if name not in self.xts:
            self.xts[name] = self.P.dram(name, shape, dt, kind="ExternalInput")
            self.ins.append(name)
        return self.xts[name]

    def xt(self, name, shape, dt, prod):
        if name in self.xts:
            return self.xts[name]
        if self.fused or name in getattr(self, "internal", ()):
            kind = "Internal"
        elif prod in self.stages:
            kind = "ExternalOutput"
            self.outs.append(name)
        else:
            kind = "ExternalInput"
            self.ins.append(name)
        self.xts[name] = self.P.dram(name, shape, dt, kind=kind)
        return self.xts[name]

    def pvv(self, name):
        o, n = self.pvoff[name]
        return self.PVt[:, o:o + n]


def setup_common(B):
    P = B.P
    B.PVt = P.sb("pvec", [128, B.nv])
    pvin = B.xin("pvec_in", [128, B.nv])
    P.dma(B.PVt[:], pvin[:])
    B.ones_bf = P.sb("ones_bf", [128, 128], BF16)
    P.memset("pool", B.ones_bf[:], 1.0)
    idf = P.sb("idf", [128, 128], F32)
    P.memset("pool", idf[:], 1.0)
    P.affine_select(idf[:], idf[:], [[-1, 128]], ALU.is_equal, 0.0, base=0, cm=1)
    B.ident_bf = P.sb("ident_bf", [128, 128], BF16)
    P.copy("dve", B.ident_bf[:], idf[:])
    B.ident_f = idf
    B.pb = [P.ps("pb%d" % i, [128, 512], F32) for i in range(7)]
    B.pbb = P.ps("pbb", [128, 1024], BF16)


def mods_layer(B, l, w_ada, mods, A):
    P = B.P
    scT = P.sb("scT%d" % l, [128, 8, 2])
    P.act(scT[:], B.pvv("c").re("p (c t) -> p c t", t=2), AF.Silu)
    pm = B.pb[0]
    wst = [P.sb("wada_st%d_%d" % (l, i), [128, 8, 768]) for i in range(2)]
    for g in range(8):
        st = wst[g % 2]
        P.dma(st[:], w_ada[l, :, g * 768:(g + 1) * 768].re("(kc p) n -> p kc n", p=128), q=("sp" if g % 2 == 0 else "pool"))
        for j in range(6):
            n = g * 6 + j
            for kc in range(8):
                P.mm(pm[:, n * 2:n * 2 + 2], st[:, kc, j * 128:(j + 1) * 128], scT[:, kc, :], start=(kc == 0), stop=(kc == 7))
    bada = B.pvv("bada%d" % l)
    for t in range(2):
        P.tt("dve", mods[:, :, t], pm[:, 0:96].re("p (n t) -> p n t", t=2)[:, :, t], bada, ALU.add)
    for i, base in ((0, 8), (1, 32)):
        for t in range(2):
            P.stt("dve", A[:, i, :, t], mods[:, base:base + 8, t], 1.0, B.pvv("ng%d%d" % (l, i)), ALU.add, ALU.mult)
    return mods, A


def norm_mod(B, Xd, c0, c1, A, Sh, out, o0, xst, tag):
    P = B.P
    w = c1 - c0
    B.nm_i += 1
    xst = B.xsts[B.nm_i % 2]
    P.dma(xst[:, :, 0:w], Xd[:, c0:c1].re("(kc p) t -> p kc t", p=128), q=("sp" if B.nm_i % 2 else "pool"))
    sq = B.nm_sqs[B.nm_i % 2]
    for kc in range(8):
        P.act(sq[:, kc, 0:w], xst[:, kc, 0:w], AF.Square)
    ps = B.pb[1] if B.nm_i % 2 else B.pb[6]
    for kc in range(8):
        P.mm(ps[:, 0:w], B.ones_bf[:], sq[:, kc, 0:w], start=(kc == 0), stop=(kc == 7))
    rstd = B.nm_rstds[B.nm_i % 2]
    P.ts("dve", rstd[:, 0:w], ps[:, 0:w], 1.0 / 1024.0, EPS, ALU.mult, ALU.add)
    P.act(rstd[:, 0:w], rstd[:, 0:w], AF.Sqrt)
    P.recip(rstd[:, 0:w], rstd[:, 0:w])
    for kc in range(8):
        tmp = B.nm_tmps[kc % 4]
        P.stt("dve", tmp[:, 0:w], xst[:, kc, 0:w], A[:, kc:kc + 1], rstd[:, 0:w], ALU.mult, ALU.mult)
        P.act(out[:, kc, o0:o0 + w], tmp[:, 0:w], AF.Identity, bias=Sh[:, kc:kc + 1])


def norm_bufs(B):
    P = B.P
    B.nm_sqs = [P.sb("nm_sq%d" % i, [128, 8, 512], BF16) for i in range(2)]
    B.nm_rstds = [P.sb("nm_rstd%d" % i, [128, 512]) for i in range(2)]
    B.nm_tmps = [P.sb("nm_tmp%d" % i, [128, 512]) for i in range(4)]
    B.xsts = [P.sb("xst%d" % i, [128, 8, 512]) for i in range(2)]
    B.nm_i = 0
    B.nm_sq, B.nm_rstd, B.xst = B.nm_sqs[0], B.nm_rstds[0], B.xsts[0]

def stage1(B, I):
    P = B.P
    Xd = B.xin("XT0", [1024, NT])
    w_ada = B.xin("w_ada", [2, 1024, 6144])
    hgw = B.xin("hg_w", [8, 128, 8 * 5 * 128])
    mods = P.sb("mods0t", [128, 48, 2])
    A = P.sb("modA0", [128, 2, 8, 2])
    with P.scope():
        mods_layer(B, 0, w_ada, mods, A)
    modsd = B.xt("mods0", [128, 96], F32, 1)
    P.dma(modsd[:], mods[:].re("p n t -> p (n t)"))
    H1 = P.sb("H1", [128, 8, NT], BF16)
    with P.scope():
        norm_bufs(B)
        for (c0, c1) in BLK5:
            t = 1 if c0 < NCX else 0
            norm_mod(B, Xd, c0, c1, A[:, 0, :, t], mods[:, 0:8, t], H1, c0, B.xst, "h1")
    lbl = B.pvv("lbl").re("p (a d h) -> p a d h", a=2, d=2)
    lb = P.sb("lb", [128, 2, 8]); oml = P.sb("oml", [128, 2, 8]); noml = P.sb("noml", [128, 2, 8])
    P.tt("dve", lb[:], lbl[:, 0], lbl[:, 1], ALU.subtract)
    P.act(lb[:], lb[:], AF.Sigmoid)
    P.ts("dve", oml[:], lb[:], -1.0, 1.0, ALU.mult, ALU.add)
    P.ts("dve", noml[:], oml[:], -1.0, None, ALU.mult)
    cm = P.sb("cm", [128, NT], F32)
    P.memset("pool", cm[:], 1.0)
    P.memset("pool", cm[:].re("p (n j) -> p n j", j=64)[:, :, 0:1], 0.0)
    ones32 = P.sb("ones32", [128, 1024], F32)
    P.memset("pool", ones32[:], 1.0)
    Mk = []
    for d in range(2):
        m = P.sb("Mk%d" % d, [128, 128], F32)
        P.memset("pool", m[:], 1.0)
        if d == 0:
            P.affine_select(m[:], m[:], [[1, 128]], ALU.is_ge, 0.0, base=0, cm=-1)
            P.memset("pool", m[0:64, 64:128], 0.0)
        else:
            P.affine_select(m[:], m[:], [[-1, 128]], ALU.is_ge, 0.0, base=0, cm=1)
            P.memset("pool", m[64:128, 0:64], 0.0)
        Mk.append(m)
    wst = [P.sb("hw_st%d" % i, [128, 5 * 128]) for i in range(2)]
    wbf = [P.sb("hw_bf%d" % i, [128, 8, 5, 128], BF16) for i in range(2)]
    qb = P.sb("qb", [128, NT], BF16)
    sg = P.sb("sg", [128, NT], BF16)
    vtok = P.sb("vtok", [128, 18, 128], BF16)
    W = 1024
    T = [P.sb("tmp%d" % i, [128, W]) for i in range(5)]
    qe = [P.sb("qe%d" % d, [128, NT], BF16) for d in range(2)]
    ke = [P.sb("ke%d" % d, [128, NT], BF16) for d in range(2)]
    qd = [P.sb("qd%d" % d, [128, NT], BF16) for d in range(2)]
    kd = [P.sb("kd%d" % d, [128, NT], BF16) for d in range(2)]
    dec = [P.sb("dec%d" % d, [128, 36]) for d in range(2)]
    qt = [P.sb("qt%d" % d, [128, NL], BF16) for d in range(2)]
    bgl = [P.sb("bgl%d" % d, [128, 1]) for d in range(2)]
    oacc = P.sb("oacc", [128, NT])
    S = [P.sb("S%d" % d, [128, 128]) for d in range(2)]
    Sbf = [P.sb("Sbf%d" % d, [128, 128], BF16) for d in range(2)]
    kdT = [P.sb("kdT%d" % d, [128, 128], BF16) for d in range(2)]
    sT = [P.sb("sT%d" % d, [128, 128], BF16) for d in range(2)]
    EPt = P.sb("EPt", [128, 2, 129])
    Sct = P.sb("Sct", [128, 2, 128])
    Yc = P.sb("Yc", [128, 8, NCX], BF16)
    osq = P.sb("osq", [128, 512], BF16)
    orst = P.sb("orst", [128, 512])
    otmp = P.sb("otmp", [128, 512])
    EPd = B.xt("EP", [128, 8, 2, 129], F32, 1)
    Sctd = B.xt("Sctx", [128, 8, 2, 128], F32, 1)
    Old = B.xt("Oloc", [8, 128, NL], F32, 1)
    Qtd = B.xt("Qt", [8, 2, 128, NL], BF16, 1)
    Sgd = B.xt("Sg", [8, 128, NL], BF16, 1)
    Ycd = B.xt("Yc", [128, 8, NCX], BF16, 1)
    pj = [B.pb[0], B.pb[1]]
    pv_ = Tl(B.pb[2].ap[:, 0:128], "pv")
    pS = [Tl(B.pb[3].ap[:, d * 128:(d + 1) * 128], "pS%d" % d) for d in range(2)]
    pO = [Tl(B.pb[4].ap[:, d * 128:(d + 1) * 128], "pO%d" % d) for d in range(2)]
    pKV = [Tl(B.pb[5].ap[:, d * 128:(d + 1) * 128], "pKV%d" % d) for d in range(2)]
    pT = [Tl(B.pbb.ap[:, d * 128:(d + 1) * 128], "pT%d" % d) for d in range(2)]
    pN = B.pb[6]
    gn = B.pvv("gn")
    segs = [(0, 256), (256, 1280), (1280, 2304)]
    njob = 0
    for h in range(8):
        wb = wbf[h % 2]
        for kc in range(8):
            st = wst[kc % 2]
            P.dma(st[:], hgw[h, :, kc * 640:(kc + 1) * 640], q=("sp" if kc % 2 == 0 else "pool"))
            P.copy("pool" if kc % 2 == 0 else "dve", wb[:, kc].re("p g n -> p (g n)"), st[:])
        def proj(g, c0, c1, pst):
            for kc in range(8):
                P.mm(pst[:, 0:c1 - c0], wb[:, kc, g, :], H1[:, kc, c0:c1], start=(kc == 0), stop=(kc == 7))
        for bi, (c0, c1) in enumerate(BLK5):
            proj(0, c0, c1, pj[0]); P.act(qb[:, c0:c1], pj[0][:, 0:c1 - c0], AF.Silu)
            proj(4, c0, c1, pj[1]); P.act(sg[:, c0:c1], pj[1][:, 0:c1 - c0], AF.Silu)
        for t in range(18):
            for kc in range(8):
                P.mm(pv_[:], H1[:, kc, t * 128:(t + 1) * 128], wb[:, kc, 3, :], start=(kc == 0), stop=(kc == 7))
            P.copy("dve", vtok[:, t, :], pv_[:])
        for d in range(2):
            sgs = segs if d == 0 else segs[::-1]
            first_lat = True
            for (c0, c1) in sgs:
                w = c1 - c0
                nch = w // 64
                sf, lf, kk, bb, ee = [t_[:, 0:w] for t_ in T]
                for (a0, a1) in [(x, min(x + 512, c1)) for x in range(c0, c1, 512)]:
                    pst = pj[njob % 2]; njob += 1
                    proj(1 + d, a0, a1, pst)
                    P.act(sf[:, a0 - c0:a1 - c0], pst[:, 0:a1 - a0], AF.Sigmoid)
                P.act(lf, sf, AF.Ln, bias=lb[:, d, h:h + 1], scale=oml[:, d, h:h + 1])
                P.ts("pool", kk, sf, noml[:, d, h:h + 1], oml[:, d, h:h + 1], ALU.mult, ALU.add)
                b_ = sf
                if d == 0:
                    P.scan("dve", b_, cm[:, 0:w], lf, 0.0, ALU.mult, ALU.add)
                else:
                    P.scan("dve", b_[:, ::-1], cm[:, 0:w], lf[:, ::-1], 0.0, ALU.mult, ALU.add)
                b3 = b_.re("p (n j) -> p n j", j=64)
                if c0 >= NCX:
                    init = 0.0 if first_lat else bgl[d][:, 0:1]
                    if d == 0:
                        P.scan("dve", bb, ones32[:, 0:w], lf, init, ALU.mult, ALU.add)
                        P.copy("pool", bgl[d][:, 0:1], bb[:, w - 1:w])
                    else:
                        P.scan("dve", bb[:, ::-1], ones32[:, 0:w], lf[:, ::-1], init, ALU.mult, ALU.add)
                        P.copy("pool", bgl[d][:, 0:1], bb[:, 0:1])
                    first_lat = False
                    P.act(ee, bb, AF.Exp)
                    P.tt("pool", qt[d][:, c0 - NCX:c1 - NCX], qb[:, c0:c1], ee, ALU.mult)
                d1 = lf
                bref = b3[:, :, 32:33].ap.broadcast_to([128, nch, 64])
                P.tt("dve", d1.re("p (n j) -> p n j", j=64), b3, Vw(b_.t, bref), ALU.subtract)
                P.act(ee, d1, AF.Exp)
                P.tt("pool", qe[d][:, c0:c1], qb[:, c0:c1], ee, ALU.mult)
                P.act(bb, d1, AF.Exp, scale=-1.0)
                P.tt("dve", ke[d][:, c0:c1], kk, bb, ALU.mult)
                P.act(ee, b_, AF.Exp)
                P.tt("pool", qd[d][:, c0:c1], qb[:, c0:c1], ee, ALU.mult)
                lastj = 63 if d == 0 else 0
                P.copy("pool", dec[d][:, c0 // 64:c1 // 64], ee.re("p (n j) -> p n j", j=64)[:, :, lastj])
                blast = b3[:, :, lastj:lastj + 1].ap.broadcast_to([128, nch, 64])
                P.tt("dve", d1.re("p (n j) -> p n j", j=64), Vw(b_.t, blast), b3, ALU.subtract)
                P.act(bb, d1, AF.Exp)
                P.tt("dve", kd[d][:, c0:c1], kk, bb, ALU.mult)
            P.act(EPt[:, d, 128:129], bgl[d][:, 0:1], AF.Exp)
            P.dma(Qtd[h, d], qt[d][:], q="pool")
        P.dma(Sgd[h], sg[:, NCX:NT], q="pool")
        seq_f = [(t, t in (0, 2)) for t in range(18)]
        seq_b = [(t, t in (17, 1)) for t in list(range(17, 1, -1)) + [1, 0]]
        for step in range(18):
            for d in range(2):
                t, fresh = (seq_f if d == 0 else seq_b)[step]
                cs = slice(t * 128, (t + 1) * 128)
                P.tr(pT[d][:], kd[d][:, cs], B.ident_bf[:])
                P.copy("act", kdT[d][:], pT[d][:])
                P.mm(pS[d][:], ke[d][:, cs], qe[d][:, cs])
                P.tt("dve", sT[d][:], pS[d][:], Mk[d][:], ALU.mult)
                for c in ((0, 1) if d == 0 else (1, 0)):
                    ps_ = slice(c * 64, (c + 1) * 64)
                    cc = slice(t * 128 + c * 64, t * 128 + (c + 1) * 64)
                    n = t * 2 + c
                    if not fresh:
                        P.mm(pO[d][:, ps_], Sbf[d][:], qd[d][:, cc], start=True, stop=False)
                    P.mm(pO[d][:, ps_], vtok[ps_, t, :], sT[d][ps_, ps_], start=fresh, stop=True)
                    P.mm(pKV[d][:], kdT[d][ps_, :], vtok[ps_, t, :])
                    if fresh:
                        P.copy("dve", S[d][:], pKV[d][:])
                    else:
                        P.stt("dve", S[d][:], S[d][:], dec[d][:, n:n + 1], pKV[d][:], ALU.mult, ALU.add)
                    P.copy("act", Sbf[d][:], S[d][:])
                    fresh = False
                firstvis = (d == 0) if t < 2 else ((d == 0) == (t <= 8))
                if firstvis:
                    P.copy("dve", oacc[:, cs], pO[d][:])
                else:
                    P.tt("dve", oacc[:, cs], oacc[:, cs], pO[d][:], ALU.add)
                if (d == 0 and t == 1) or (d == 1 and t == 0):
                    P.copy("pool", Sct[:, d, :], S[d][:])
                if (d == 0 and t == 17) or (d == 1 and t == 2):
                    P.copy("pool", EPt[:, d, 0:128], S[d][:])
        P.dma(EPd[:, h], EPt[:])
        P.dma(Sctd[:, h], Sct[:])
        P.dma(Old[h], oacc[:, NCX:NT])
        P.act(osq[:, 0:NCX], oacc[:, 0:NCX], AF.Square)
        P.mm(pN[:, 0:NCX], B.ones_bf[:], osq[:, 0:NCX])
        P.ts("dve", orst[:, 0:NCX], pN[:, 0:NCX], 1.0 / 128.0, EPS, ALU.mult, ALU.add)
        P.act(orst[:, 0:NCX], orst[:, 0:NCX], AF.Sqrt)
        P.recip(orst[:, 0:NCX], orst[:, 0:NCX])
        P.stt("dve", otmp[:, 0:NCX], oacc[:, 0:NCX], gn[:, 0:1], orst[:, 0:NCX], ALU.mult, ALU.mult)
        P.tt("pool", Yc[:, h, :], otmp[:, 0:NCX], sg[:, 0:NCX], ALU.mult)
    P.dma(Ycd[:], Yc[:], final=not B.fused)

def load_mods(B, l, prod):
    P = B.P
    modsd = B.xt("mods%d" % l, [128, 96], F32, prod)
    mods = P.sb("modsL%d" % l, [128, 48, 2])
    P.dma(mods[:].re("p n t -> p (n t)"), modsd[:])
    return mods


def stage2(B, I):
    P = B.P
    Xd = B.xin("XT0", [1024, NT])
    wo = B.xin("hg_wo", [128, 8, 1024])
    EPa = B.xt("EP_all", [8, 128, 8, 2, 129], F32, "x1")
    Sctd = B.xt("Sctx", [128, 8, 2, 128], F32, 1)
    Old = B.xt("Oloc", [8, 128, NL], F32, 1)
    Qtd = B.xt("Qt", [8, 2, 128, NL], BF16, 1)
    Sgd = B.xt("Sg", [8, 128, NL], BF16, 1)
    Ycd = B.xt("Yc", [128, 8, NCX], BF16, 1)
    X1d = B.xt("XT1", [1024, NT], F32, 2)
    HXd = B.xt("HX0", [128, 2, 8], F32, 2)
    mods = load_mods(B, 0, 1)
    Y = P.sb("Y", [128, 8, NT], BF16)
    P.dma(Y[:, :, 0:NCX], Ycd[:])
    msk = B.pvv("msk")
    gn = B.pvv("gn")
    ol = [P.sb("ol%d" % i, [128, NL]) for i in range(2)]
    qt = [[P.sb("qtl%d_%d" % (i, d), [128, NL], BF16) for d in range(2)] for i in range(2)]
    sgl = [P.sb("sgl%d" % i, [128, NL], BF16) for i in range(2)]
    eph = [P.sb("eph%d" % i, [128, 8, 2, 129]) for i in range(2)]
    sct = [P.sb("sctl%d" % i, [128, 2, 128]) for i in range(2)]
    coef = P.sb("coef", [128, 2, 8])
    Sin = [P.sb("Sin%d" % d, [128, 128]) for d in range(2)]
    Sinb = [P.sb("Sinb%d" % d, [128, 128], BF16) for d in range(2)]
    tmpE = P.sb("tmpE", [128, 128])
    osq = P.sb("osq2", [128, 512], BF16)
    orst = P.sb("orst2", [128, 512])
    otmp = P.sb("otmp2", [128, 512])
    pC, pN = B.pb[0], B.pb[1]
    for h in range(8):
        i = h % 2
        P.dma(ol[i][:], Old[h])
        for d in range(2):
            P.dma(qt[i][d][:], Qtd[h, d], q="pool")
        P.dma(sgl[i][:], Sgd[h], q="pool")
        P.dma(eph[i][:], EPa[:, :, h].re("r p d n -> p r d n"))
        P.dma(sct[i][:], Sctd[:, h])
        for d in range(2):
            mo = 0 if d == 0 else 16
            P.tt("dve", coef[:, d, :], eph[i][:, :, d, 128], msk[:, mo:mo + 8], ALU.mult)
            P.tt("dve", coef[:, d, :], coef[:, d, :], msk[:, mo + 8:mo + 16], ALU.add)
            P.copy("dve", Sin[d][:], sct[i][:, d, :])
            for j in (range(8) if d == 0 else range(7, -1, -1)):
                P.ts("pool", tmpE[:], eph[i][:, j, d, 0:128], msk[:, mo + j:mo + j + 1], None, ALU.mult)
                P.stt("dve", Sin[d][:], Sin[d][:], coef[:, d, j:j + 1], tmpE[:], ALU.mult, ALU.add)
            P.copy("act", Sinb[d][:], Sin[d][:])
        for bk in range(4):
            cs = slice(bk * 512, (bk + 1) * 512)
            P.mm(pC[:], Sinb[0][:], qt[i][0][:, cs], start=True, stop=False)
            P.mm(pC[:], Sinb[1][:], qt[i][1][:, cs], start=False, stop=True)
            P.tt("dve", ol[i][:, cs], ol[i][:, cs], pC[:], ALU.add)
            P.act(osq[:], ol[i][:, cs], AF.Square)
            P.mm(pN[:], B.ones_bf[:], osq[:])
            P.ts("dve", orst[:], pN[:], 1.0 / 128.0, EPS, ALU.mult, ALU.add)
            P.act(orst[:], orst[:], AF.Sqrt)
            P.recip(orst[:], orst[:])
            P.stt("dve", otmp[:], ol[i][:, cs], gn[:, 0:1], orst[:], ALU.mult, ALU.mult)
            P.tt("pool", Y[:, h, NCX + bk * 512:NCX + (bk + 1) * 512], otmp[:], sgl[i][:, cs], ALU.mult)
    wob = P.sb("wob", [128, 8, 1024], BF16)
    wst = P.sb("wo_st", [128, 2, 1024])
    for g in range(4):
        P.dma(wst[:], wo[:, g * 2:(g + 1) * 2, :])
        P.copy("pool" if g % 2 else "act", wob[:, g * 2:(g + 1) * 2, :], wst[:])
    xr = [P.sb("xr%d" % i, [128, NT]) for i in range(2)]
    hxs = P.sb("hxs", [128, 2, 8])
    pw = [B.pb[2], B.pb[3]]
    nj = 0
    for n in range(8):
        x = xr[n % 2]
        P.dma(x[:], Xd[n * 128:(n + 1) * 128, :])
        for (c0, c1) in BLK5:
            t = 1 if c0 < NCX else 0
            p_ = pw[nj % 2]; nj += 1
            for kc in range(8):
                P.mm(p_[:, 0:c1 - c0], wob[:, kc, n * 128:(n + 1) * 128], Y[:, kc, c0:c1], start=(kc == 0), stop=(kc == 7))
            P.stt("dve", x[:, c0:c1], p_[:, 0:c1 - c0], mods[:, 16 + n, t:t + 1], x[:, c0:c1], ALU.mult, ALU.add)
        P.dma(X1d[n * 128:(n + 1) * 128, :], x[:], q="pool")
        P.copy("pool", hxs[:, 0, n:n + 1], x[:, NCX:NCX + 1])
        P.copy("pool", hxs[:, 1, n:n + 1], x[:, NT - 1:NT])
    P.dma(HXd[:], hxs[:])

def norm_sb(B, src, w, A, Sh, out, o0):
    P = B.P
    sq = B.nm_sq
    for kc in range(8):
        P.act(sq[:, kc, 0:w], src[:, kc, 0:w], AF.Square)
    ps = B.pb[1]
    for kc in range(8):
        P.mm(ps[:, 0:w], B.ones_bf[:], sq[:, kc, 0:w], start=(kc == 0), stop=(kc == 7))
    rstd = B.nm_rstd
    P.ts("dve", rstd[:, 0:w], ps[:, 0:w], 1.0 / 1024.0, EPS, ALU.mult, ALU.add)
    P.act(rstd[:, 0:w], rstd[:, 0:w], AF.Sqrt)
    P.recip(rstd[:, 0:w], rstd[:, 0:w])
    for kc in range(8):
        tmp = B.nm_tmps[kc % 4]
        P.stt("dve", tmp[:, 0:w], src[:, kc, 0:w], A[:, kc:kc + 1], rstd[:, 0:w], ALU.mult, ALU.mult)
        P.act(out[:, kc, o0:o0 + w], tmp[:, 0:w], AF.Identity, bias=Sh[:, kc:kc + 1])


def stage_ffn(B, l, sid, xin_name, xin_prod, hx_name, xout_name, last):
    P = B.P
    Xd = B.xt(xin_name, [1024, NT], F32, xin_prod)
    HXa = B.xt(hx_name + "_all", [8, 128, 2, 8], F32, "x")
    wup = B.xin("wup%d" % l, [22, 128, 2048])
    wdn = B.xin("wdn%d" % l, [128, 22, 1024])
    Ad = B.xt("Aff%d" % l, [22, 128, NT], BF16, sid)
    mods = load_mods(B, l, 1 if l == 0 else 4)
    msk = B.pvv("msk"); edge = B.pvv("edge")
    A2 = P.sb("A2", [128, 8, 2])
    for t in range(2):
        P.stt("dve", A2[:, :, t], mods[:, 32:40, t], 1.0, B.pvv("ng%d1" % l), ALU.add, ALU.mult)
    NW = NT + 4
    CO, LO = 1, 259
    H2 = P.sb("H2", [128, 8, NW], BF16)
    P.memset("pool", H2[:, :, 0:1], 0.0)
    P.memset("pool", H2[:, :, 257:258], 0.0)
    with P.scope():
        norm_bufs(B)
        for (c0, c1) in BLK5:
            t = 1 if c0 < NCX else 0
            o0 = CO + c0 if c0 < NCX else LO + (c0 - NCX)
            norm_mod(B, Xd, c0, c1, A2[:, :, t], mods[:, 24:32, t], H2, o0, B.xst, "h2")
        hxa = P.sb("hxa", [128, 8, 2, 8])
        P.dma(hxa[:], HXa[:].re("r p t k -> p r t k"))
        hx = P.sb("hx", [128, 8, 2])
        tmph = P.sb("tmph", [128, 8])
        for side, (mo, tt_) in enumerate(((32, 1), (40, 0))):
            P.ts("dve", hx[:, :, side], hxa[:, 0, tt_, :], msk[:, mo:mo + 1], None, ALU.mult)
            for j in range(1, 8):
                P.stt("dve", hx[:, :, side], hxa[:, j, tt_, :], msk[:, mo + j:mo + j + 1], hx[:, :, side], ALU.mult, ALU.add)
        hh = P.sb("hh", [128, 8, 2], BF16)
        norm_sb(B, hx, 2, A2[:, :, 0], mods[:, 24:32, 0], hh, 0)
        P.ts("pool", H2[:, :, LO - 1], hh[:, :, 0], edge[:, 0:1], None, ALU.mult)
        P.ts("pool", H2[:, :, LO + NL], hh[:, :, 1], edge[:, 1:2], None, ALU.mult)
    fcw = B.pvv("fcw%d" % l).re("p (k j) -> p k j", k=3)
    fcb = B.pvv("fcb%d" % l)
    blocks = [(0, 512), (512, 1024), (1024, 1536), (1536, 2048), (2048, NW)]
    with P.scope():
        wst = [P.sb("wu_st%d" % i, [128, 2048]) for i in range(2)]
        wbf = [P.sb("wu_bf%d" % i, [128, 8, 2, 128], BF16) for i in range(2)]
        uu_ = [[P.sb("u%d_%d" % (i, b_), [128, NW]) for i in range(2)] for b_ in range(2)]
        cvv_ = [[P.sb("cv%d_%d" % (i, b_), [128, NW]) for i in range(2)] for b_ in range(2)]
        aj = [P.sb("aj%d" % i, [128, NW], BF16) for i in range(2)]
        pj = [B.pb[0], B.pb[2], B.pb[3], B.pb[4]]
        nj = 0
        for j in range(22):
            st, wb = wst[j % 2], wbf[j % 2]
            P.dma(st[:], wup[j], q=("sp" if j % 2 == 0 else "pool"))
            P.copy("pool", wb[:].re("p a b c -> p (a b c)"), st[:])
            W_ = NW - 2
            u = uu_[j % 2]; cv = cvv_[j % 2]
            for t in range(2):
                ch = t * 22 + j
                for (c0, c1) in blocks:
                    p_ = pj[nj % 4]; nj += 1
                    for kc in range(8):
                        P.mm(p_[:, 0:c1 - c0], wb[:, kc, t, :], H2[:, kc, c0:c1], start=(kc == 0), stop=(kc == 7))
                    P.copy("act", u[t][:, c0:c1], p_[:, 0:c1 - c0])
                    c1b = min(c1, W_)
                    P.act(cv[t][:, c0 + 1:c1b + 1], p_[:, 0:c1b - c0], AF.Identity, bias=fcb[:, ch:ch + 1], scale=fcw[:, 0, ch:ch + 1])
                P.stt("dve", cv[t][:, 1:1 + W_], u[t][:, 1:1 + W_], fcw[:, 1, ch:ch + 1], cv[t][:, 1:1 + W_], ALU.mult, ALU.add)
                P.stt("dve", cv[t][:, 1:1 + W_], u[t][:, 2:2 + W_], fcw[:, 2, ch:ch + 1], cv[t][:, 1:1 + W_], ALU.mult, ALU.add)
            P.act(cv[0][:, 1:NW - 1], cv[0][:, 1:NW - 1], AF.Silu)
            a_ = aj[j % 2]
            P.tt("dve", a_[:, 1:NW - 1], cv[0][:, 1:NW - 1], cv[1][:, 1:NW - 1], ALU.mult)
            P.dma(Ad[j, :, 0:NCX], a_[:, CO:CO + NCX], q="sp")
            P.dma(Ad[j, :, NCX:NT], a_[:, LO:LO + NL], q="pool")
    Xo = B.xt(xout_name, [1024, NT], F32, sid)
    with P.scope():
        wd = P.sb("wd", [128, 22, 1024], BF16)
        wst = [P.sb("wd_st%d" % i, [128, 2, 1024]) for i in range(2)]
        for g in range(11):
            P.dma(wst[g % 2][:], wdn[:, g * 2:(g + 1) * 2, :], q=("sp" if g % 2 == 0 else "pool"))
            P.copy("pool" if g % 2 else "act", wd[:, g * 2:(g + 1) * 2, :], wst[g % 2][:])
        ab = [P.sb("ab%d" % i, [128, 22, 512], BF16) for i in range(2)]
        xb = [P.sb("xb%d" % i, [128, 512]) for i in range(4)]
        pw = [B.pb[0], B.pb[2], B.pb[3], B.pb[4]]
        if last:
            fsq = P.sb("fsq", [128, 8, 512], BF16)
            xk = P.sb("xk", [128, 8, 512])
            frs = P.sb("frs", [128, 512])
            OUTd = B.xt("OUT", [1024, NL], F32, sid)
            fg = B.pvv("fg")
        else:
            HRo = B.xt("HR", [128, 8, 192], F32, sid)
        nj = 0
        blks = BLK5[1:] if last else BLK5
        for bi, (c0, c1) in enumerate(blks):
            w = c1 - c0
            t = 1 if c0 < NCX else 0
            a_ = ab[bi % 2]
            P.dma(a_[:, :, 0:w], Ad[:, :, c0:c1].re("j p t -> p j t"))
            for n in range(8):
                x = xb[nj % 4]
                p_ = pw[nj % 4]; nj += 1
                P.dma(x[:, 0:w], Xd[n * 128:(n + 1) * 128, c0:c1], q="pool")
                for j in range(22):
                    P.mm(p_[:, 0:w], wd[:, j, n * 128:(n + 1) * 128], a_[:, j, 0:w], start=(j == 0), stop=(j == 21))
                if last:
                    P.stt("dve", xk[:, n, 0:w], p_[:, 0:w], mods[:, 40 + n, t:t + 1], x[:, 0:w], ALU.mult, ALU.add)
                else:
                    P.stt("dve", x[:, 0:w], p_[:, 0:w], mods[:, 40 + n, t:t + 1], x[:, 0:w], ALU.mult, ALU.add)
                    P.dma(Xo[n * 128:(n + 1) * 128, c0:c1], x[:, 0:w], q="sp")
                    if c0 == NCX:
                        P.dma(HRo[:, n, 0:64], x[:, 0:64], q="sp")
                    if c1 == NT:
                        P.dma(HRo[:, n, 64:192], x[:, w - 128:w], q="sp")
            if last:
                for n in range(8):
                    P.act(fsq[:, n, :], xk[:, n, :], AF.Square)
                ps = B.pb[1]
                for n in range(8):
                    P.mm(ps[:], B.ones_bf[:], fsq[:, n, :], start=(n == 0), stop=(n == 7))
                P.ts("dve", frs[:], ps[:], 1.0 / 1024.0, EPS, ALU.mult, ALU.add)
                P.act(frs[:], frs[:], AF.Sqrt)
                P.recip(frs[:], frs[:])
                for n in range(8):
                    P.stt("dve", xk[:, n, :], xk[:, n, :], fg[:, n:n + 1], frs[:], ALU.mult, ALU.mult)
                P.dma(OUTd[:, c0 - NCX:c1 - NCX].re("(n p) t -> p n t", p=128), xk[:])


def stage3(B, I):
    stage_ffn(B, 0, 3, "XT1", 2, "HX0", "XT2", False)


def stage6(B, I):
    stage_ffn(B, 1, 6, "XT3", 5, "HX1", "XT4", True)

def lat_cm(v):
    return v.re("p (r c) -> p c r", c=64)


def stage4(B, I):
    P = B.P
    Xd = B.xt("XT2", [1024, NT], F32, 3)
    HRa = B.xt("HR_all", [8, 128, 8, 192], F32, "x")
    w_ada = B.xin("w_ada", [2, 1024, 6144])
    lwi = B.xin("lru_wi", [8, 128, 8, 2, 128])
    wax = B.xin("lru_wax", [4, 128, 2, 2, 2, 256])
    mods = P.sb("mods1t", [128, 48, 2])
    A = P.sb("modA1", [128, 2, 8, 2])
    with P.scope():
        mods_layer(B, 1, w_ada, mods, A)
    modsd = B.xt("mods1", [128, 96], F32, 4)
    P.dma(modsd[:], mods[:].re("p n t -> p (n t)"))
    msk = B.pvv("msk"); edge = B.pvv("edge")
    H1 = P.sb("H1b", [128, 8, NT], BF16)
    hh = P.sb("hhr", [128, 8, 192], BF16)
    with P.scope():
        norm_bufs(B)
        for (c0, c1) in BLK5:
            t = 1 if c0 < NCX else 0
            norm_mod(B, Xd, c0, c1, A[:, 0, :, t], mods[:, 0:8, t], H1, c0, B.xst, "h1")
        hra = P.sb("hra", [128, 8, 192])
        hsel = P.sb("hsel", [128, 8, 192])
        for j in range(8):
            P.dma(hra[:], HRa[j])
            for (lo, hi, mo) in ((0, 64, 56), (64, 192, 48)):
                if j == 0:
                    P.ts("dve", hsel[:, :, lo:hi], hra[:, :, lo:hi], msk[:, mo:mo + 1], None, ALU.mult)
                else:
                    P.stt("dve", hsel[:, :, lo:hi], hra[:, :, lo:hi], msk[:, mo + j:mo + j + 1], hsel[:, :, lo:hi], ALU.mult, ALU.add)
        norm_sb(B, hsel, 192, A[:, 0, :, 0], mods[:, 0:8, 0], hh, 0)
    lam = B.pvv("lam").re("p (d c) -> p d c", d=2)
    ca = P.sb("ca", [128, 2, 8]); ca2 = P.sb("ca2", [128, 2, 8])
    P.act(ca[:], lam, AF.Exp, scale=-1.0)
    P.act(ca[:], ca[:], AF.Ln, bias=1.0)
    P.ts("dve", ca2[:], ca[:], -16.0, None, ALU.mult)
    P.ts("dve", ca[:], ca[:], -8.0, None, ALU.mult)
    ba = B.pvv("ba").re("p (d c) -> p d c", d=2)
    bx = B.pvv("bx").re("p (d c) -> p d c", d=2)
    cw = B.pvv("cw").re("p (k c) -> p k c", k=4)
    cb = B.pvv("cb")
    segm = P.sb("segm", [128, NL])
    P.memset("pool", segm[:], 1.0)
    P.memset("pool", segm[:].re("p (c r) -> p c r", r=32)[:, :, 0:1], 0.0)
    SMd = B.xt("SM", [128, 8, 2, 2, 64], F32, 4)
    HCd = B.xt("HCs", [128, 8, 2], F32, 4)
    HLd = B.xt("HL", [8, 2, 2, 128, NL], F32, 4)
    Ggd = B.xt("Gg", [8, 128, NL], BF16, 4)
    SMt = P.sb("SMt", [128, 8, 2, 2, 64])
    HCt = P.sb("HCt", [128, 8, 2])
    wst = P.sb("lw_st", [128, 8, 2, 128])
    wbf = [P.sb("lw_bf%d" % i, [128, 8, 2, 128], BF16) for i in range(2)]
    wxst = P.sb("wax_st", [128, 2048])
    wxbf = P.sb("wax_bf", [128, 2, 2, 2, 256], BF16)
    xbuf = P.sb("xbuf", [128, 64, 35])
    cbuf = P.sb("cbuf", [128, 259])
    P.memset("pool", cbuf[:], 0.0)
    xhal = P.sb("xhal", [128, 192])
    xc = [P.sb("xc%d" % i, [128, NT]) for i in range(2)]
    xcb = [P.sb("xcb%d" % i, [128, NT], BF16) for i in range(2)]
    ggb = P.sb("ggb", [128, NL], BF16)
    rr = P.sb("rr", [128, NT]); ii = P.sb("ii", [128, NT]); gt = ii[:, 0:NL]; gtm = rr[:, 0:NL]; aa = P.sb("aa", [128, NT]); uu = P.sb("uu", [128, NT])
    am = P.sb("am", [128, NL]); as_ = P.sb("as_", [128, NL])
    hl = P.sb("hl", [128, NL]); Al = P.sb("Al", [128, NL]); hcx = P.sb("hcx", [128, NCX])
    pj = [B.pb[0], B.pb[2], B.pb[3], B.pb[4]]
    nj = 0
    cblocks = [(i * 16, (i + 1) * 16) for i in range(4)]
    for nb in range(4):
        P.dma(wxst[:], wax[nb].re("p d t k j -> p (d t k j)"))
        P.copy("pool", wxbf[:].re("p d t k j -> p (d t k j)"), wxst[:])
        for sub in range(2):
            cc = nb * 2 + sub
            wb = wbf[cc % 2]
            P.dma(wst[:], lwi[cc])
            P.copy("pool", wb[:], wst[:])
            for (ca_, cb_) in cblocks:
                p_ = pj[nj % 4]; nj += 1
                for kc in range(8):
                    P.mm(p_[:], wb[:, kc, 0, :], lat_cm(H1[:, kc, NCX:NT])[:, ca_:cb_, :], start=(kc == 0), stop=(kc == 7))
                P.copy("act", gt[:, ca_ * 32:cb_ * 32], p_[:])
                p_ = pj[nj % 4]; nj += 1
                for kc in range(8):
                    P.mm(p_[:], wb[:, kc, 1, :], lat_cm(H1[:, kc, NCX:NT])[:, ca_:cb_, :], start=(kc == 0), stop=(kc == 7))
                P.copy("act", xbuf[:, ca_:cb_, 2:34], p_[:].re("p (c r) -> p c r", r=32))
            p_ = pj[nj % 4]; nj += 1
            for kc in range(8):
                P.mm(p_[:, 0:256], wb[:, kc, 1, :], H1[:, kc, 0:NCX], start=(kc == 0), stop=(kc == 7))
            P.copy("act", cbuf[:, 2:258], p_[:, 0:256])
            p_ = pj[nj % 4]; nj += 1
            for kc in range(8):
                P.mm(p_[:, 0:192], wb[:, kc, 1, :], hh[:, kc, :], start=(kc == 0), stop=(kc == 7))
            P.copy("act", xhal[:], p_[:, 0:192])
            for (slot, lo) in ((0, 64), (1, 128)):
                P.ts("dve", xbuf[:, :, slot], xhal[:, lo:lo + 64], edge[:, 4:5], None, ALU.mult)
                P.stt("dve", xbuf[:, 1:64, slot], xhal[:, lo:lo + 63], edge[:, 2:3], xbuf[:, 1:64, slot], ALU.mult, ALU.add)
            P.ts("dve", xbuf[:, :, 34], xhal[:, 0:64], edge[:, 5:6], None, ALU.mult)
            P.stt("dve", xbuf[:, 0:63, 34], xhal[:, 1:64], edge[:, 3:4], xbuf[:, 0:63, 34], ALU.mult, ALU.add)
            xl = xc[sub][:, NCX:NT].re("p (c r) -> p c r", r=32)
            P.ts("dve", xl, xbuf[:, :, 0:32], cw[:, 0, cc:cc + 1], cb[:, cc:cc + 1], ALU.mult, ALU.add)
            for k_ in range(1, 4):
                P.stt("dve", xl, xbuf[:, :, k_:k_ + 32], cw[:, k_, cc:cc + 1], xl, ALU.mult, ALU.add)
            xcc = xc[sub][:, 0:NCX]
            P.ts("dve", xcc, cbuf[:, 0:256], cw[:, 0, cc:cc + 1], cb[:, cc:cc + 1], ALU.mult, ALU.add)
            for k_ in range(1, 4):
                P.stt("dve", xcc, cbuf[:, k_:k_ + 256], cw[:, k_, cc:cc + 1], xcc, ALU.mult, ALU.add)
            P.copy("pool", xcb[sub][:], xc[sub][:])
            P.act(gtm, gt, AF.Square)
            P.ts("pool", gtm, gtm, 0.044715, 1.0, ALU.mult, ALU.add)
            P.tt("pool", gtm, gtm, gt, ALU.mult)
            P.act(gtm, gtm, AF.Sigmoid, scale=1.5957691216057308)
            P.tt("pool", ggb[:], gtm, gt, ALU.mult)
            P.dma(Ggd[cc], ggb[:], q="pool")
        for d in range(2):
            for js in range(2):
                cc = nb * 2 + js
                for ty, dst, bias in ((0, rr, ba), (1, ii, bx)):
                    for (c0, c1) in BLK5:
                        p_ = pj[nj % 4]; nj += 1
                        for ks in range(2):
                            P.mm(p_[:, 0:c1 - c0], wxbf[:, d, ty, ks, js * 128:(js + 1) * 128], xcb[ks][:, c0:c1], start=(ks == 0), stop=(ks == 1))
                        P.act(dst[:, c0:c1], p_[:, 0:c1 - c0], AF.Sigmoid, bias=bias[:, d, cc:cc + 1])
                P.act(aa[:], rr[:], AF.Exp, scale=ca[:, d, cc:cc + 1])
                P.act(rr[:], rr[:], AF.Exp, scale=ca2[:, d, cc:cc + 1])
                P.act(rr[:], rr[:], AF.Sqrt, bias=1.0, scale=-1.0)
                P.tt("dve", uu[:], ii[:], xc[js][:], ALU.mult)
                P.tt("dve", uu[:], uu[:], rr[:], ALU.mult)
                al, ul = aa[:, NCX:NT], uu[:, NCX:NT]
                if d == 0:
                    P.scan("dve", hcx[:], aa[:, 0:NCX], uu[:, 0:NCX], 0.0, ALU.mult, ALU.add)
                    P.copy("pool", HCt[:, cc, 0:1], hcx[:, NCX - 1:NCX])
                    P.tt("pool", am[:], al, segm[:], ALU.mult)
                    P.tt("pool", as_[:], al, am[:], ALU.subtract)
                    P.scan("dve", hl[:], am[:], ul, 0.0, ALU.mult, ALU.add)
                    P.scan("dve", Al[:], am[:], as_[:], 0.0, ALU.mult, ALU.add)
                    e_ = 31
                else:
                    P.scan("dve", hcx[:, ::-1], aa[:, 0:NCX][:, ::-1], uu[:, 0:NCX][:, ::-1], 0.0, ALU.mult, ALU.add)
                    P.copy("pool", HCt[:, cc, 1:2], hcx[:, 0:1])
                    P.tt("pool", am[:, ::-1], al[:, ::-1], segm[:], ALU.mult)
                    P.tt("pool", as_[:], al, am[:], ALU.subtract)
                    P.scan("dve", hl[:, ::-1], am[:, ::-1], ul[:, ::-1], 0.0, ALU.mult, ALU.add)
                    P.scan("dve", Al[:, ::-1], am[:, ::-1], as_[:, ::-1], 0.0, ALU.mult, ALU.add)
                    e_ = 0
                P.copy("pool", SMt[:, cc, d, 0, :], Al[:].re("p (c r) -> p c r", r=32)[:, :, e_])
                P.copy("pool", SMt[:, cc, d, 1, :], hl[:].re("p (c r) -> p c r", r=32)[:, :, e_])
                P.dma(HLd[cc, d, 0], hl[:], q="sp")
                P.dma(HLd[cc, d, 1], Al[:], q="pool")
    P.dma(SMd[:], SMt[:])
    P.dma(HCd[:], HCt[:])


def stage5(B, I):
    P = B.P
    Xd = B.xt("XT2", [1024, NT], F32, 3)
    SMa = B.xt("SM_all", [8, 128, 8, 2, 2, 64], F32, "x")
    HCd = B.xt("HCs", [128, 8, 2], F32, 4)
    HLd = B.xt("HL", [8, 2, 2, 128, NL], F32, 4)
    Ggd = B.xt("Gg", [8, 128, NL], BF16, 4)
    wo = B.xin("lru_wo", [128, 8, 1024])
    X3d = B.xt("XT3", [1024, NT], F32, 5)
    HXd = B.xt("HX1", [128, 2, 8], F32, 5)
    mods = load_mods(B, 1, 4)
    msk = B.pvv("msk"); k1h = B.pvv("k1h")
    HCt = P.sb("HCt5", [128, 8, 2])
    P.dma(HCt[:], HCd[:])
    Yl = P.sb("Yl", [128, 8, NL], BF16)
    sma = P.sb("sma", [128, 8, 2, 2, 64])
    seq = P.sb("seq", [128, 2, 2, 64, 4])
    G = P.sb("G", [128, 258])
    Hin = [P.sb("Hin%d" % d, [128, 64]) for d in range(2)]
    hl = [P.sb("hl5_%d" % i, [128, NL]) for i in range(2)]
    Al = [P.sb("Al5_%d" % i, [128, NL]) for i in range(2)]
    gg = P.sb("gg5", [128, NL], BF16)
    ysum = P.sb("ysum", [128, NL])
    for cc in range(8):
        P.dma(sma[:], SMa[:, :, cc].re("r p d t c -> p r d t c"))
        P.dma(gg[:], Ggd[cc], q="pool")
        for kq in range(4):
            P.ts("dve", seq[:, :, :, :, kq], sma[:, kq], msk[:, 72 + kq:73 + kq], None, ALU.mult)
            P.stt("dve", seq[:, :, :, :, kq], sma[:, 4 + kq], msk[:, 76 + kq:77 + kq], seq[:, :, :, :, kq], ALU.mult, ALU.add)
        for d in range(2):
            P.dma(hl[d][:], HLd[cc, d, 0], q="sp")
            P.dma(Al[d][:], HLd[cc, d, 1], q="pool")
            Pq = seq[:, d, 0].re("p c k -> p (c k)")
            Eq = seq[:, d, 1].re("p c k -> p (c k)")
            if d == 0:
                P.copy("pool", G[:, 0:1], HCt[:, cc, 0:1])
                P.scan("dve", G[:, 1:257], Pq, Eq, HCt[:, cc, 0:1], ALU.mult, ALU.add)
                Gs = G[:, 0:256].re("p (c k) -> p c k", k=4)
            else:
                P.copy("pool", G[:, 257:258], HCt[:, cc, 1:2])
                P.scan("dve", G[:, 1:257][:, ::-1], Pq[:, ::-1], Eq[:, ::-1], HCt[:, cc, 1:2], ALU.mult, ALU.add)
                Gs = G[:, 2:258].re("p (c k) -> p c k", k=4)
            P.ts("dve", Hin[d][:], Gs[:, :, 0], k1h[:, 0:1], None, ALU.mult)
            for kq in range(1, 4):
                P.stt("dve", Hin[d][:], Gs[:, :, kq], k1h[:, kq:kq + 1], Hin[d][:], ALU.mult, ALU.add)
            hb = Hin[d][:].ap.unsqueeze(2).broadcast_to([128, 64, 32])
            A3 = Al[d][:].re("p (c r) -> p c r", r=32)
            P.tt("pool", A3, A3, Vw(Hin[d], hb), ALU.mult)
            P.tt("dve", hl[d][:], hl[d][:], Al[d][:], ALU.add)
        P.tt("dve", ysum[:], hl[0][:], hl[1][:], ALU.add)
        P.tt("pool", Yl[:, cc, :], ysum[:], gg[:], ALU.mult)
    wob = P.sb("lwob", [128, 8, 1024], BF16)
    wst = P.sb("lwo_st", [128, 2, 1024])
    for g in range(4):
        P.dma(wst[:], wo[:, g * 2:(g + 1) * 2, :])
        P.copy("pool" if g % 2 else "act", wob[:, g * 2:(g + 1) * 2, :], wst[:])
    xr = [P.sb("xr5_%d" % i, [128, NT]) for i in range(2)]
    hxs = P.sb("hxs5", [128, 2, 8])
    pw = [B.pb[2], B.pb[3]]
    nj = 0
    for n in range(8):
        x = xr[n % 2]
        P.dma(x[:], Xd[n * 128:(n + 1) * 128, :])
        xl = lat_cm(x[:, NCX:NT])
        for bi in range(4):
            p_ = pw[nj % 2]; nj += 1
            for kc in range(8):
                P.mm(p_[:], wob[:, kc, n * 128:(n + 1) * 128], Yl[:, kc, bi * 512:(bi + 1) * 512], start=(kc == 0), stop=(kc == 7))
            P.stt("dve", xl[:, bi * 16:(bi + 1) * 16, :], p_[:].re("p (c r) -> p c r", r=32), mods[:, 16 + n, 0:1], xl[:, bi * 16:(bi + 1) * 16, :], ALU.mult, ALU.add)
        P.dma(X3d[n * 128:(n + 1) * 128, :], x[:], q="pool")
        P.copy("pool", hxs[:, 0, n:n + 1], x[:, NCX:NCX + 1])
        P.copy("pool", hxs[:, 1, n:n + 1], x[:, NT - 1:NT])
    P.dma(HXd[:], hxs[:])

STAGE_FNS = {}


def host_layouts(inputs):
    I = {k: np.asarray(v) for k, v in inputs.items()}
    shared = {}
    shared["w_ada"] = np.ascontiguousarray(I["w_ada"], dtype=np.float32)
    w = I["hg_w_in"][0].reshape(8, 128, 5, 8, 128)
    shared["hg_w"] = np.ascontiguousarray(w.transpose(3, 1, 0, 2, 4).reshape(8, 128, 8 * 5 * 128))
    shared["hg_wo"] = np.ascontiguousarray(I["hg_w_out"][0].reshape(8, 128, 1024).transpose(1, 0, 2))
    for l in range(2):
        wu = I["ffn_w_up"][l].reshape(8, 128, 2, 22, 128)
        shared["wup%d" % l] = np.ascontiguousarray(wu.transpose(3, 1, 0, 2, 4).reshape(22, 128, 2048))
        shared["wdn%d" % l] = np.ascontiguousarray(I["ffn_w_down"][l].reshape(22, 128, 1024).transpose(1, 0, 2))
    li = I["lru_w_in"][0].reshape(8, 128, 2, 8, 128)
    shared["lru_wi"] = np.ascontiguousarray(li.transpose(3, 1, 0, 2, 4))
    wa = I["lru_wa"][0].reshape(2, 4, 2, 128, 256)
    wx = I["lru_wx"][0].reshape(2, 4, 2, 128, 256)
    wax = np.stack([wa, wx], axis=0)
    shared["lru_wax"] = np.ascontiguousarray(wax.transpose(2, 4, 1, 0, 3, 5))
    shared["lru_wo"] = np.ascontiguousarray(I["lru_w_out"][0].reshape(8, 128, 1024).transpose(1, 0, 2))
    percore = []
    pvs = None
    for core in range(8):
        b, k = core // 4, core % 4
        d = {}
        d["XT0"] = np.ascontiguousarray(np.concatenate([I["ctx"][b].T, I["x"][b, k * NL:(k + 1) * NL].T], axis=1))
        pv = pv_layout(I, core)
        d["pvec_in"] = pv.pack()
        pvs = pv
        percore.append(d)
    return I, shared, percore, pvs


def run_stage(stage_ids, fn_list, store, shared, pvs, fused=False):
    B = Bld(stage_ids, fused, pvs.off, pvs.n)
    setup_common(B)
    for fn in fn_list:
        fn(B, None)
    B.P.emit()
    in_maps = []
    for c in range(8):
        m = {}
        for n in B.ins:
            m[n] = shared[n] if n in shared else store[c][n]
        in_maps.append(m)
    res = run_bass_kernel_spmd(B.nc, in_maps, core_ids=list(range(8)))
    for c in range(8):
        for n in B.outs:
            store[c][n] = np.asarray(res.results[c][n])
    return B


def gather(store, name):
    g = np.ascontiguousarray(np.stack([store[c][name] for c in range(8)], axis=0))
    for c in range(8):
        store[c][name + "_all"] = g


def kernel_unfused(**inputs):
    I, shared, store, pvs = host_layouts(inputs)
    run_stage([1], [stage1], store, shared, pvs)
    gather(store, "EP")
    run_stage([2], [stage2], store, shared, pvs)
    gather(store, "HX0")
    run_stage([3], [stage3], store, shared, pvs)
    gather(store, "HR")
    run_stage([4], [stage4], store, shared, pvs)
    gather(store, "SM")
    run_stage([5], [stage5], store, shared, pvs)
    gather(store, "HX1")
    run_stage([6], [stage6], store, shared, pvs)
    out = np.empty((2, 8192, 1024), np.float32)
    for c in range(8):
        b, k = c // 4, c % 4
        out[b, k * NL:(k + 1) * NL, :] = store[c]["OUT"].T
    return out


EXCH = {"EP": [128, 8 * 2 * 129], "HX0": [128, 16], "HR": [128, 8 * 192], "SM": [128, 8 * 2 * 2 * 64], "HX1": [128, 16]}
EXCH_FULL = {"EP": ([128, 8, 2, 129], [8, 128, 8, 2, 129]), "HX0": ([128, 2, 8], [8, 128, 2, 8]),
             "HR": ([128, 8, 192], [8, 128, 8, 192]), "SM": ([128, 8, 2, 2, 64], [8, 128, 8, 2, 2, 64]),
             "HX1": ([128, 2, 8], [8, 128, 2, 8])}


def build_fused(pvs):
    B = Bld([1, 2, 3, 4, 5, 6], True, pvs.off, pvs.n)
    P = B.P
    setup_common(B)
    B.xts["OUT"] = P.dram("OUT", [1024, NL], F32, kind="ExternalOutput")
    B.outs.append("OUT")

    def exch(name):
        shp, shp_all = EXCH_FULL[name]
        src = B.xt(name, shp, F32, 0)
        dst = B.xt(name + "_all", shp_all, F32, 0)
        n = 1
        for d_ in shp[1:]:
            n *= d_
        P.allgather(Vw(dst, _flat2(dst.ap, 8 * 128, n)), Vw(src, _flat2(src.ap, 128, n)))

    seq = [(stage1, "EP"), (stage2, "HX0"), (stage3, "HR"), (stage4, "SM"), (stage5, "HX1"), (stage6, None)]
    for fn, ex in seq:
        with P.scope():
            fn(B, None)
        if ex is not None:
            exch(ex)
    P.emit()
    return B


def _flat2(ap, rows, n):
    nd = len(ap.shape)
    names = " ".join("a%d" % i for i in range(nd))
    if ap.shape[0] == rows:
        return ap.rearrange("%s -> a0 (%s)" % (names, " ".join("a%d" % i for i in range(1, nd))))
    return ap.rearrange("%s -> (a0 a1) (%s)" % (names, " ".join("a%d" % i for i in range(2, nd))))


def kernel(**inputs):
    I, shared, store, pvs = host_layouts(inputs)
    B = build_fused(pvs)
    in_maps = []
    for c in range(8):
        m = {}
        for n in B.ins:
            m[n] = shared[n] if n in shared else store[c][n]
        in_maps.append(m)
    res = run_bass_kernel_spmd(B.nc, in_maps, core_ids=list(range(8)))
    out = np.empty((2, 8192, 1024), np.float32)
    for c in range(8):
        b, k = c // 4, c % 4
        out[b, k * NL:(k + 1) * NL, :] = np.asarray(res.results[c]["OUT"]).T
    return out


STAGES = {1: (None, "EP"), 2: (None, "HX0"), 3: (None, "HR"), 4: (None, "SM"), 5: (None, "HX1"), 6: (None, None)}


def run_group(ids, store, shared, pvs):
    fns = {1: stage1, 2: stage2, 3: stage3, 4: stage4, 5: stage5, 6: stage6}
    B = Bld(ids, False, pvs.off, pvs.n)
    P = B.P
    internal = set()
    for i in ids[:-1]:
        ex = STAGES[i][1]
        internal.add(ex); internal.add(ex + "_all")
    B.internal = internal
    setup_common(B)
    for i in ids:
        with P.scope():
            fns[i](B, None)
        ex = STAGES[i][1]
        if ex is not None and i != ids[-1]:
            shp, shp_all = EXCH_FULL[ex]
            src = B.xt(ex, shp, F32, 0)
            dst = B.xt(ex + "_all", shp_all, F32, 0)
            n = 1
            for d_ in shp[1:]:
                n *= d_
            P.allgather(Vw(dst, _flat2(dst.ap, 8 * 128, n)), Vw(src, _flat2(src.ap, 128, n)))
    P.emit()
    in_maps = []
    for c in range(8):
        m = {}
        for n in B.ins:
            m[n] = shared[n] if n in shared else store[c][n]
        in_maps.append(m)
    res = run_bass_kernel_spmd(B.nc, in_maps, core_ids=list(range(8)))
    for c in range(8):
        for n in B.outs:
            store[c][n] = np.asarray(res.results[c][n])
    ex = STAGES[ids[-1]][1]
    if ex is not None:
        gather(store, ex)


def kernel_groups(groups, **inputs):
    I, shared, store, pvs = host_layouts(inputs)
    for g in groups:
        run_group(g, store, shared, pvs)
    out = np.empty((2, 8192, 1024), np.float32)
    for c in range(8):
        b, k = c // 4, c % 4
        out[b, k * NL:(k + 1) * NL, :] = store[c]["OUT"].T
    return out
```
